# Optimizing a Trainium2 kernel written in Bass

```python
import jax, jax.numpy as jnp
from jax import lax
import numpy as np

D_MODEL = 1024
BATCH = 8
SEQ = 2048
DEPTH = 4
DEC_BATCH = 128
DEC_SEQ = 4
PAST_LEN = 16384
PAGE_SIZE = 128

N_AB = (DEPTH + 1) // 2
N_C = DEPTH // 2
D_POOL = D_MODEL // 2
POOL_WINDOWS = (2, 4, 8, 16)
N_POOL_GROUPS = len(POOL_WINDOWS)
POOL_GROUP = D_POOL // N_POOL_GROUPS
POOL_HIST = max(POOL_WINDOWS) - 1
D_GMLP = D_MODEL // 2
GMLP_CHUNK = 128
H_GMLP = 4
GMLP_HEAD = D_GMLP // H_GMLP
D_IN_AB = D_POOL + 2 * D_GMLP
D_MIX_AB = D_POOL + D_GMLP
D_CONV = D_MODEL
CONV_WIDTH = 31
CONV_HIST = CONV_WIDTH - 1
D_FF = 4 * D_MODEL
EPS = 1e-6

kernel_name = 'pool_gmlp_conformer_hybrid_step'


def rmsnorm(x, g):
    xf = x.astype(jnp.float32)
    y = xf * lax.rsqrt(jnp.mean(xf * xf, axis=-1, keepdims=True) + EPS)
    return (y * g.astype(jnp.float32)).astype(x.dtype)


def layernorm(x, g, b):
    xf = x.astype(jnp.float32)
    mu = jnp.mean(xf, axis=-1, keepdims=True)
    var = jnp.mean(jnp.square(xf - mu), axis=-1, keepdims=True)
    y = (xf - mu) * lax.rsqrt(var + EPS)
    return (y * g.astype(jnp.float32) + b.astype(jnp.float32)).astype(x.dtype)


def multiscale_pool(a, hist, p0, w_grp, scale):
    B, L, _ = a.shape
    ext = jnp.concatenate([hist.astype(a.dtype), a], axis=1)
    cs = jnp.concatenate([jnp.zeros((B, 1, D_POOL), jnp.float32),
                          jnp.cumsum(ext.astype(jnp.float32), axis=1)], axis=1)
    end = cs[:, POOL_HIST + 1:]
    pos = p0 + jnp.arange(L, dtype=jnp.int32)
    means = []
    for g, w in enumerate(POOL_WINDOWS):
        sl = slice(g * POOL_GROUP, (g + 1) * POOL_GROUP)
        start = cs[:, POOL_HIST + 1 - w:POOL_HIST + 1 - w + L, sl]
        cnt = jnp.minimum(pos + 1, w).astype(jnp.float32)[None, :, None]
        means.append((end[..., sl] - start) / cnt)
    pooled = jnp.stack(means, axis=2) - a.astype(jnp.float32).reshape(B, L, N_POOL_GROUPS, POOL_GROUP)
    mixed = jnp.einsum('blgc,gcd->blgd', pooled.astype(a.dtype), w_grp).reshape(B, L, D_POOL)
    return mixed * scale, ext[:, -POOL_HIST:]


def chunk_spatial_gate(v, w_s, b_s):
    B, L, _ = v.shape
    n_chunks = -(-L // GMLP_CHUNK)
    Lp = n_chunks * GMLP_CHUNK
    vp = jnp.pad(v, ((0, 0), (0, Lp - L), (0, 0))).reshape(B, n_chunks, GMLP_CHUNK, H_GMLP, GMLP_HEAD)
    mask = jnp.tril(jnp.ones((GMLP_CHUNK, GMLP_CHUNK), dtype=bool))
    wm = jnp.where(mask[None], w_s, 0).astype(v.dtype)
    mixed = jnp.einsum('hts,bnshd->bnthd', wm, vp) + jnp.transpose(b_s)[None, None, :, :, None].astype(v.dtype)
    return mixed.reshape(B, Lp, D_GMLP)[:, :L]


def causal_dwconv(g, hist, w_dw, b_dw):
    ext = jnp.concatenate([hist.astype(g.dtype), g], axis=1)
    y = lax.conv_general_dilated(ext, w_dw[:, None, :].astype(g.dtype), window_strides=(1,),
                                 padding='VALID', dimension_numbers=('NWC', 'WIO', 'NWC'),
                                 feature_group_count=D_CONV)
    return y + b_dw.astype(g.dtype), ext[:, -CONV_HIST:]


def pool_gate_layer(x, hist, p0, g_norm, w_in, pool_w, pool_scale, ln_g, ln_b, w_s, b_s, w_out):
    h = rmsnorm(x, g_norm)
    z = h @ w_in
    a = z[..., :D_POOL]
    uv = jax.nn.gelu(z[..., D_POOL:], approximate=False)
    u, v = uv[..., :D_GMLP], uv[..., D_GMLP:]
    v = layernorm(v, ln_g, ln_b)
    pool_out, new_hist = multiscale_pool(a, hist, p0, pool_w, pool_scale)
    gate_out = u * chunk_spatial_gate(v, w_s, b_s)
    out = jnp.concatenate([pool_out, gate_out], axis=-1) @ w_out
    return x + out, new_hist, v


def conformer_conv_layer(x, hist, g_norm, w_pw1, w_dw, b_dw, ln_g, ln_b, w_pw2):
    h = rmsnorm(x, g_norm)
    z = h @ w_pw1
    glu = z[..., :D_CONV] * jax.nn.sigmoid(z[..., D_CONV:])
    c, new_hist = causal_dwconv(glu, hist, w_dw, b_dw)
    c = jax.nn.silu(layernorm(c, ln_g, ln_b))
    return x + c @ w_pw2, new_hist


def sqrelu_ffn(x, g_norm, w1, w2):
    h = rmsnorm(x, g_norm)
    return x + jnp.square(jax.nn.relu(h @ w1)) @ w2


def setup_inputs(seed: int = 0) -> dict:
    key = jax.random.key(seed)
    ks = jax.random.split(key, 24)
    f32 = jnp.float32

    def nrm(k, shape, s):
        return s * jax.random.normal(k, shape, f32)

    return {
        'x_prompt': nrm(ks[0], (BATCH, SEQ, D_MODEL), 1.0),
        'x_sample': nrm(ks[1], (DEC_BATCH, DEC_SEQ, D_MODEL), 1.0),
        'state_pool': nrm(ks[2], (N_AB, DEC_BATCH, POOL_HIST, D_POOL), 1.0),
        'state_conv': nrm(ks[3], (N_C, DEC_BATCH, CONV_HIST, D_CONV), 0.5),
        'norm_mix': 1.0 + nrm(ks[4], (DEPTH, D_MODEL), 0.02),
        'norm_ffn': 1.0 + nrm(ks[5], (DEPTH, D_MODEL), 0.02),
        'norm_final': 1.0 + nrm(ks[6], (D_MODEL,), 0.02),
        'ab_w_in': nrm(ks[7], (N_AB, D_MODEL, D_IN_AB), D_MODEL ** -0.5),
        'ab_pool_w': nrm(ks[8], (N_AB, N_POOL_GROUPS, POOL_GROUP, POOL_GROUP), POOL_GROUP ** -0.5),
        'ab_pool_scale': 1.0 + nrm(ks[9], (N_AB, D_POOL), 0.02),
        'ab_ln_g': 1.0 + nrm(ks[10], (N_AB, D_GMLP), 0.02),
        'ab_ln_b': nrm(ks[11], (N_AB, D_GMLP), 0.02),
        'ab_ws': nrm(ks[12], (N_AB, H_GMLP, GMLP_CHUNK, GMLP_CHUNK), GMLP_CHUNK ** -0.5),
        'ab_bs': 1.0 + nrm(ks[13], (N_AB, H_GMLP, GMLP_CHUNK), 0.02),
        'ab_w_out': nrm(ks[14], (N_AB, D_MIX_AB, D_MODEL), D_MIX_AB ** -0.5),
        'c_w_pw1': nrm(ks[15], (N_C, D_MODEL, 2 * D_CONV), D_MODEL ** -0.5),
        'c_w_dw': nrm(ks[16], (N_C, CONV_WIDTH, D_CONV), CONV_WIDTH ** -0.5),
        'c_b_dw': nrm(ks[17], (N_C, D_CONV), 0.02),
        'c_ln_g': 1.0 + nrm(ks[18], (N_C, D_CONV), 0.02),
        'c_ln_b': nrm(ks[19], (N_C, D_CONV), 0.02),
        'c_w_pw2': nrm(ks[20], (N_C, D_CONV, D_MODEL), D_CONV ** -0.5),
        'ffn_w1': nrm(ks[21], (DEPTH, D_MODEL, D_FF), D_MODEL ** -0.5),
        'ffn_w2': nrm(ks[22], (DEPTH, D_FF, D_MODEL), D_FF ** -0.5),
    }


def reference(x_prompt, x_sample, state_pool, state_conv, norm_mix, norm_ffn, norm_final,
              ab_w_in, ab_pool_w, ab_pool_scale, ab_ln_g, ab_ln_b, ab_ws, ab_bs, ab_w_out,
              c_w_pw1, c_w_dw, c_b_dw, c_ln_g, c_ln_b, c_w_pw2, ffn_w1, ffn_w2):
    xp, xs = x_prompt, x_sample
    bp = xp.shape[0]
    pool_p, pool_s, conv_p, conv_s, gate_v_s = [], [], [], [], []
    for layer in range(DEPTH):
        i = layer // 2
        if layer % 2 == 0:
            w = (norm_mix[layer], ab_w_in[i], ab_pool_w[i], ab_pool_scale[i], ab_ln_g[i], ab_ln_b[i],
                 ab_ws[i], ab_bs[i], ab_w_out[i])
            xp, hp, _ = pool_gate_layer(xp, jnp.zeros((bp, POOL_HIST, D_POOL), xp.dtype), 0, *w)
            xs, hs, vs = pool_gate_layer(xs, state_pool[i], PAST_LEN, *w)
            pool_p.append(hp)
            pool_s.append(hs)
            gate_v_s.append(vs)
        else:
            w = (norm_mix[layer], c_w_pw1[i], c_w_dw[i], c_b_dw[i], c_ln_g[i], c_ln_b[i], c_w_pw2[i])
            xp, hp = conformer_conv_layer(xp, jnp.zeros((bp, CONV_HIST, D_CONV), xp.dtype), *w)
            xs, hs = conformer_conv_layer(xs, state_conv[i], *w)
            conv_p.append(hp)
            conv_s.append(hs)
        xp = sqrelu_ffn(xp, norm_ffn[layer], ffn_w1[layer], ffn_w2[layer])
        xs = sqrelu_ffn(xs, norm_ffn[layer], ffn_w1[layer], ffn_w2[layer])
    y_prompt = rmsnorm(xp, norm_final)
    y_sample = rmsnorm(xs, norm_final)
    return (y_prompt, y_sample, jnp.stack(pool_p), jnp.stack(pool_s), jnp.stack(conv_p), jnp.stack(conv_s), jnp.stack(gate_v_s))
```

```python
import numpy as np
import concourse.bass as bass
import concourse.mybir as mybir
from concourse.bass_utils import run_bass_kernel_spmd

F32 = mybir.dt.float32
BF16 = mybir.dt.bfloat16
AF = mybir.ActivationFunctionType
ALU = mybir.AluOpType

NCORES = 8
D = 1024
SEQ = 2048
NS = 64
NSEQ = 16
NTOK = SEQ + NS
DEPTH = 4
EPS = 1e-6
APAD = 30
AW = APAD + SEQ + NS
CW = 31
SLOT = 16384

PC_NMIX = 0
PC_NFFN = 32
PC_NFIN = 64
PC_PSCALE = 72
PC_BDW = 80
PC_LNG = 96
PC_LNB = 112
PC_WDW = 128
PC_N = 128 + 2 * 8 * 31


class Sched:
    def __init__(self, nc):
        self.nc = nc
        self.eng = dict(pe=nc.tensor, act=nc.scalar, dve=nc.vector, pool=nc.gpsimd, sp=nc.sync)
        self.sems = {}
        self.cnt = {}
        self.seen = {e: {} for e in self.eng}
        self.lastw = {}
        self.readers = {}
        for e in ("pe", "act", "dve", "pool"):
            self._sem(e)

    def _sem(self, src):
        if src not in self.sems:
            self.sems[src] = self.nc.alloc_semaphore("s_" + src)
            self.cnt[src] = 0
        return self.sems[src]

    def _deps(self, reads, writes):
        need = {}

        def add(ev):
            if ev is not None:
                s, c = ev
                if need.get(s, 0) < c:
                    need[s] = c
        for r in reads:
            add(self.lastw.get(r))
        for r in writes:
            add(self.lastw.get(r))
            for s, c in self.readers.get(r, {}).items():
                add((s, c))
        return need

    def _wait(self, e, need):
        for s, c in need.items():
            if e == "pe" and s == "pe":
                continue
            if self.seen[e].get(s, 0) >= c:
                continue
            self.eng[e].wait_ge(self.sems[s], c)
            self.seen[e][s] = c

    def _commit(self, src, c, reads, writes):
        for r in reads:
            d = self.readers.setdefault(r, {})
            if d.get(src, 0) < c:
                d[src] = c
        for r in writes:
            self.lastw[r] = (src, c)
            self.readers[r] = {}

    def op(self, e, reads, writes, fn, after=()):
        self._wait(e, self._deps(reads, list(writes) + list(after)))
        ins = fn(self.eng[e])
        self.cnt[e] += 1
        ins.then_inc(self.sems[e], 1)
        self._commit(e, self.cnt[e], reads, writes)

    def dma(self, q, stream, reads, writes, fn):
        self._sem(stream)
        self._wait(q, self._deps(reads, writes))
        for ins in fn(self.eng[q]):
            self.cnt[stream] += 16
            ins.then_inc(self.sems[stream], 16)
        self._commit(stream, self.cnt[stream], reads, writes)

    def fence_snapshot(self):
        return {s: c for s, c in self.cnt.items()}

    def fence_apply(self, snap):
        for e in ("act", "dve", "pool", "sp"):
            self._wait(e, {s: c for s, c in snap.items() if c > 0})

    def final_wait(self, e="sp"):
        self._wait(e, {s: c for s, c in self.cnt.items() if c > 0})


class Tl:
    def __init__(self, x0, n, sample=False):
        self.x0, self.n, self.sample = x0, n, sample
        self.a0 = APAD + x0
        self.blocks = ["s"] if sample else list(range(x0 // 256, (x0 + n) // 256))

    def xk(self, c=None):
        cs = range(8) if c is None else [c]
        return [("X", b, cc) for b in self.blocks for cc in cs]

    def ak(self, c=None):
        cs = range(8) if c is None else [c]
        return [("A", b, cc) for b in self.blocks for cc in cs]


def build_program(depth_run=DEPTH):
    nc = bass.Bass("TRN2", target_bir_lowering=False)

    def din(name, shape):
        return nc.dram_tensor(name, list(shape), F32, kind="ExternalInput").ap()

    def dout(name, shape):
        return nc.dram_tensor(name, list(shape), F32, kind="ExternalOutput").ap()

    xT = din("xT", [D, NTOK])
    spT = din("spT", [2, 512, NSEQ, 15])
    sp_tm = din("sp_tm", [2, NSEQ, 15, 512])
    scT = din("scT", [2, D, NSEQ, 30])
    sc_tm = din("sc_tm", [2, NSEQ, 30, D])
    pcols_d = din("pcols", [128, PC_N])
    lng_d = din("ab_ln_g", [2, 512])
    lnb_d = din("ab_ln_b", [2, 512])
    wsT_d = din("wsT", [2, 4, 128, 128])
    wsTs_d = din("wsTs", [2, 4, 64, 64])
    bs_d = din("ab_bs", [2, 4, 128])
    ident_d = din("ident", [128, 128])
    maskT_d = din("maskT", [128, 128])
    bdmask_d = din("bdmask", [64, 64])
    invcnt_d = din("invcnt", [128, 4, 16])
    w_in_d = din("ab_w_in", [2, D, 1536])
    pool_w_d = din("ab_pool_w", [2, 4, 128, 128])
    w_out_d = din("ab_w_out", [2, D, D])
    w_pw1_d = din("c_w_pw1", [2, D, 2 * D])
    w_pw2_d = din("c_w_pw2", [2, D, D])
    w1_d = din("ffn_w1", [DEPTH, D, 4 * D])
    w2_d = din("ffn_w2", [DEPTH, 4 * D, D])

    yT = dout("yT", [D, NTOK])
    npp = dout("npp", [2, 512, 15])
    nps_old = dout("nps_old", [2, NSEQ, 11, 512])
    nps_new = dout("nps_new", [2, 512, NS])
    ncp = dout("ncp", [2, D, 30])
    ncs_old = dout("ncs_old", [2, NSEQ, 26, D])
    ncs_new = dout("ncs_new", [2, D, NS])
    gv = dout("gv", [2, NS, 512])

    S = Sched(nc)

    def sb(name, shape, dt=F32):
        return nc.alloc_sbuf_tensor(name, list(shape), dt)

    X = sb("X", [128, 8, NTOK])
    A = sb("A", [128, 8, AW], BF16)
    ring = sb("ring", [128, 2, SLOT], BF16)
    pcols = sb("pcols_sb", [128, PC_N])
    ones_b = sb("ones_b", [128, 128], BF16)
    ident_b = sb("ident_b", [128, 128], BF16)
    eps_t = sb("eps_t", [128, 1])
    maskT = sb("maskT_sb", [128, 128])
    bdmask = sb("bdmask_sb", [64, 64])
    invcnt = sb("invcnt_sb", [128, 4, 16])
    lng_bc = sb("lng_bc", [128, 512])
    lnb_bc = sb("lnb_bc", [128, 512])
    wmT = sb("wmT", [128, 4, 128], BF16)
    wmTs = sb("wmTs", [64, 4, 64], BF16)
    bsrow = sb("bsrow", [1, 4, 128])
    bs_hi = sb("bs_hi", [1, 4, 128], BF16)
    bs_lo = sb("bs_lo", [1, 4, 128], BF16)
    WORKB = 31488
    work = sb("work", [128, WORKB], mybir.dt.uint8)
    psum = nc.alloc_psum_tensor("psum", [128, 8, 512], F32)

    def wview(off, shape, dt):
        esz = 4 if dt == F32 else 2
        n = int(np.prod(shape))
        assert off % 4 == 0
        ap = work[:, off:off + n * esz].bitcast(dt)
        if len(shape) == 2:
            ap = ap.rearrange("p (a b) -> p a b", b=shape[1])
        elif len(shape) == 3:
            ap = ap.rearrange("p (a b c) -> p a b c", b=shape[1], c=shape[2])
        off2 = off + n * esz
        assert off2 <= WORKB, (off2, WORKB)
        return ap, off2

    bank_ctr = [0]

    def bank():
        b = bank_ctr[0] % 8
        bank_ctr[0] += 1
        return b

    def ld_consts(e):
        out = []
        out.append(e.dma_start(out=pcols[:], in_=pcols_d))
        out.append(e.dma_start(out=maskT[:], in_=maskT_d))
        out.append(e.dma_start(out=bdmask[:], in_=bdmask_d))
        out.append(e.dma_start(out=invcnt[:], in_=invcnt_d))
        return out
    S.dma("sp", "d_const", [], [("const",)], ld_consts)
    S.dma("pool", "d_ident", [], [("identb",)], lambda e: [e.dma_start(out=ident_b[:], in_=ident_d)])
    S.op("dve", [], [("ones",)], lambda e: e.memset(ones_b[:], 1.0))
    S.op("dve", [], [("eps",)], lambda e: e.memset(eps_t[:], EPS))
    S.op("pool", [], [("A", "pad", c) for c in range(8)], lambda e: e.memset(A[:, :, 0:APAD], 0.0))

    xv = xT.rearrange("(c p) t -> p c t", p=128)
    XT512 = [Tl(512 * t, 512) for t in range(4)] + [Tl(SEQ, NS, sample=True)]
    XT256 = [Tl(256 * s, 256) for s in range(8)] + [Tl(SEQ, NS, sample=True)]
    for i, tl in enumerate(XT256[:2] + XT512[1:]):
        S.dma("sp", "d_x%d" % i, [], tl.xk(),
              lambda e, tl=tl: [e.dma_start(out=X[:, :, tl.x0:tl.x0 + tl.n], in_=xv[:, :, tl.x0:tl.x0 + tl.n])])

    slot_ctr = [0]

    NPART = 3

    def slot_keys(s, part=None):
        return [("ring", s, j) for j in (range(NPART) if part is None else (part,))]

    def load_unit(parts):
        s = slot_ctr[0] % 2
        slot_ctr[0] += 1
        S._wait("pool", S._deps([], slot_keys(s)))
        for j in range(NPART):
            st = "d_w%d_%d" % (s, j)
            S._sem(st)
            if j < len(parts):
                off, (k, n), src = parts[j][:3]
                dst = ring[:, s, off:off + k * n].rearrange("p (k n) -> p k n", n=n)
                if len(parts[j]) > 3:
                    lo, hi = parts[j][3]
                    dst = dst[:, :, lo:hi]
                S.cnt[st] += 16
                nc.gpsimd.dma_start(out=dst, in_=src).then_inc(S.sems[st], 16)
                S._commit(st, S.cnt[st], [], [("ring", s, j)])
            else:
                st0 = "d_w%d_0" % s
                S._commit(st0, S.cnt[st0], [], [("ring", s, j)])
        return s

    def rview(s, off, k, n):
        return ring[:, s, off:off + k * n].rearrange("p (k n) -> p k n", n=n)

    def norm_stats(tl, sq, rt, rstd):
        n = tl.n
        S.op("act", tl.xk(), [("sq",)],
             lambda e: e.activation(out=sq[:, :, :n], in_=X[:, :, tl.x0:tl.x0 + n], func=AF.Square))
        b = bank()

        def mm(e):
            for c in range(8):
                ins = e.matmul(psum[:, b, :n], lhsT=ones_b[:], rhs=sq[:, c, :n], start=(c == 0), stop=(c == 7))
            return ins
        S.op("pe", [("sq",), ("ones",)], [("ps", b)], mm)
        S.op("act", [("ps", b), ("eps",)], [("rt",)],
             lambda e: e.activation(out=rt[:, :n], in_=psum[:, b, :n], func=AF.Sqrt, scale=1.0 / D, bias=eps_t[:, 0:1]))
        S.op("dve", [("rt",)], [("rstd",)], lambda e: e.reciprocal(out=rstd[:, :n], in_=rt[:, :n]))

    def norm_apply(tl, rstd, gidx, dst_fn, dst_keys_fn, rkey=("rstd",), chunks=range(8)):
        n = tl.n
        for c in chunks:
            S.op("dve", tl.xk(c) + [rkey, ("const",)], dst_keys_fn(c),
                 lambda e, c=c: e.scalar_tensor_tensor(out=dst_fn(c), in0=X[:, c, tl.x0:tl.x0 + n],
                                                       scalar=pcols[:, gidx + c:gidx + c + 1], in1=rstd[:, :n],
                                                       op0=ALU.mult, op1=ALU.mult))

    def proj_out(tl, slot, woff):
        n = tl.n
        w = rview(slot, woff, 8, D)
        for o in range(8):
            b = bank()

            def mm(e, o=o, b=b):
                for k in range(8):
                    ins = e.matmul(psum[:, b, :n], lhsT=w[:, k, o * 128:(o + 1) * 128], rhs=A[:, k, tl.a0:tl.a0 + n],
                                   start=(k == 0), stop=(k == 7))
                return ins
            S.op("pe", tl.ak() + slot_keys(slot), [("ps", b)], mm)
            S.op("dve", [("ps", b)] + tl.xk(o), tl.xk(o),
                 lambda e, o=o, b=b: e.tensor_tensor(out=X[:, o, tl.x0:tl.x0 + n], in0=psum[:, b, :n],
                                                     in1=X[:, o, tl.x0:tl.x0 + n], op=ALU.add))

    def ffn_units(layer):
        w1v = w1_d[layer].rearrange("(kc p) n -> p kc n", p=128)
        w2v = w2_d[layer].rearrange("(jc p) n -> p jc n", p=128)

        def unit(q):
            return [(0, (8, 1024), w1v[:, :, q * 1024:(q + 1) * 1024]),
                    (8192, (8, 1024), w2v[:, q * 8:(q + 1) * 8, :])]
        return unit

    def ffn_preload(layer):
        unit = ffn_units(layer)
        return [load_unit(unit(0)), load_unit(unit(1))]

    def ffn_layer(layer, next_unit_loader, after_last_O=None, pre_slots=None):
        off = 0
        sq, off = wview(off, [8, 256], BF16)
        rt, off = wview(off, [256], F32)
        rstd, off = wview(off, [256], F32)
        hid0, off = wview(off, [8, 512], BF16)
        hid1, off = wview(off, [8, 512], BF16)
        rb0, off = wview(off, [512], F32)
        rb1, off = wview(off, [512], F32)
        hids = [hid0, hid1]
        rbs = [rb0, rb1]
        unit = ffn_units(layer)

        hctr = [0]

        def H(q, tl, slot, hooks=None):
            n = tl.n
            hb = hctr[0] % 2
            hctr[0] += 1
            w1 = rview(slot, 0, 8, 1024)
            for j in range(8):
                if hooks and j in hooks:
                    hooks[j]()
                b = bank()

                def mm(e, j=j, b=b):
                    for k in range(8):
                        ins = e.matmul(psum[:, b, :n], lhsT=w1[:, k, j * 128:(j + 1) * 128],
                                       rhs=A[:, k, tl.a0:tl.a0 + n], start=(k == 0), stop=(k == 7))
                    return ins
                S.op("pe", tl.ak() + slot_keys(slot, 0), [("ps", b)], mm)
                rb = rbs[j % 2]
                S.op("act", [("ps", b)], [("rb", j % 2)],
                     lambda e, b=b, rb=rb: e.activation(out=rb[:, :n], in_=psum[:, b, :n], func=AF.Relu))
                S.op("dve", [("ps", b), ("rb", j % 2)], [("hid", hb, j)],
                     lambda e, b=b, rb=rb, j=j: e.tensor_tensor(out=hids[hb][:, j, :n], in0=psum[:, b, :n],
                                                                in1=rb[:, :n], op=ALU.mult))
            return hb

        def O(tl, slot, hb):
            n = tl.n
            w2 = rview(slot, 8192, 8, 1024)
            for o in range(8):
                b = bank()

                def mm(e, o=o, b=b):
                    for j in range(8):
                        ins = e.matmul(psum[:, b, :n], lhsT=w2[:, j, o * 128:(o + 1) * 128], rhs=hids[hb][:, j, :n],
                                       start=(j == 0), stop=(j == 7))
                    return ins
                S.op("pe", [("hid", hb, j) for j in range(8)] + slot_keys(slot, 1), [("ps", b)], mm)
                S.op("dve", [("ps", b)] + tl.xk(o), tl.xk(o),
                     lambda e, o=o, b=b: e.tensor_tensor(out=X[:, o, tl.x0:tl.x0 + n], in0=psum[:, b, :n],
                                                         in1=X[:, o, tl.x0:tl.x0 + n], op=ALU.add))

        tiles = XT512
        slots = [None] * 4
        if pre_slots is None:
            pre_slots = [load_unit(unit(0)), load_unit(unit(1))]
        slots[0], slots[1] = pre_slots
        pend = None
        fin_pend = []
        for q in range(4):
            if q + 1 < 4:
                pass
            for ti, tl in enumerate(tiles):
                hooks = None
                if q == 0:
                    def n_sq(st):
                        S.op("act", st.xk(), [("sq",)],
                             lambda e: e.activation(out=sq[:, :, :st.n], in_=X[:, :, st.x0:st.x0 + st.n], func=AF.Square))

                    def n_mid(st):
                        n_ = st.n
                        b = bank()

                        def mm(e):
                            for c in range(8):
                                ins = e.matmul(psum[:, b, :n_], lhsT=ones_b[:], rhs=sq[:, c, :n_], start=(c == 0),
                                               stop=(c == 7))
                            return ins
                        S.op("pe", [("sq",), ("ones",)], [("ps", b)], mm)
                        S.op("act", [("ps", b), ("eps",)], [("rt",)],
                             lambda e: e.activation(out=rt[:, :n_], in_=psum[:, b, :n_], func=AF.Sqrt, scale=1.0 / D,
                                                    bias=eps_t[:, 0:1]))
                        S.op("dve", [("rt",)], [("rstd",)], lambda e: e.reciprocal(out=rstd[:, :n_], in_=rt[:, :n_]))

                    def n_app(st, chunks):
                        norm_apply(st, rstd, PC_NFFN + layer * 8,
                                   lambda c: A[:, c, st.a0:st.a0 + st.n], lambda c: st.ak(c), chunks=chunks)

                    def subs_of(tn):
                        return [tn] if tn.sample else [XT256[tn.x0 // 256], XT256[tn.x0 // 256 + 1]]
                    if ti == 0:
                        for st in subs_of(tiles[0]):
                            n_sq(st)
                            n_mid(st)
                            n_app(st, range(8))
                    if ti + 1 < len(tiles):
                        sb_ = subs_of(tiles[ti + 1])
                        two = len(sb_) > 1
                        n_sq(sb_[0])
                        hooks = {1: (lambda sb_=sb_: n_mid(sb_[0])),
                                 2: (lambda sb_=sb_: n_app(sb_[0], range(0, 4))),
                                 3: (lambda sb_=sb_, two=two: (n_app(sb_[0], range(4, 8)), n_sq(sb_[1]) if two else None)),
                                 5: (lambda sb_=sb_, two=two: n_mid(sb_[1]) if two else None),
                                 6: (lambda sb_=sb_, two=two: n_app(sb_[1], range(0, 4)) if two else None),
                                 7: (lambda sb_=sb_, two=two: n_app(sb_[1], range(4, 8)) if two else None)}
                hb = H(q, tl, slots[q], hooks)
                if pend is not None:
                    O(*pend[:3])
                    if pend[3] == 3 and after_last_O is not None:
                        if fin_pend:
                            after_last_O(fin_pend.pop(0), sq, rt, rstd)
                        fin_pend.append(pend[0])
                pend = (tl, slots[q], hb, q)
                if ti == 0 and q >= 1:
                    if q + 1 < 4:
                        slots[q + 1] = load_unit(unit(q + 1))
                    else:
                        next_unit_loader()
        O(*pend[:3])
        if after_last_O is not None:
            fin_pend.append(pend[0])
            while fin_pend:
                after_last_O(fin_pend.pop(0), sq, rt, rstd)

    def ab_params(i):
        S.dma("sp", "d_abp", [], [("abp",)], lambda e: [
            e.dma_start(out=lng_bc[:], in_=lng_d[i].partition_broadcast(128)),
            e.dma_start(out=lnb_bc[:], in_=lnb_d[i].partition_broadcast(128)),
            e.dma_start(out=bsrow[:], in_=bs_d[i:i + 1]),
        ])
        S.dma("pool", "d_ws", [], [("wmT",), ("wmTs",)], lambda e: [
            e.dma_start(out=wmT[:], in_=wsT_d[i].rearrange("h s t -> s h t")),
            e.dma_start(out=wmTs[:], in_=wsTs_d[i].rearrange("h s t -> s h t")),
        ])
        mbc = maskT[:].unsqueeze(1).to_broadcast([128, 4, 128])
        S.op("dve", [("const",), ("wmT",)], [("wmT",)],
             lambda e: e.tensor_tensor(out=wmT[:], in0=wmT[:], in1=mbc, op=ALU.mult))
        mbs = bdmask[:].unsqueeze(1).to_broadcast([64, 4, 64])
        S.op("dve", [("const",), ("wmTs",)], [("wmTs",)],
             lambda e: e.tensor_tensor(out=wmTs[:], in0=wmTs[:], in1=mbs, op=ALU.mult))
        S.op("dve", [("abp",)], [("bs_hi",)], lambda e: e.tensor_copy(bs_hi[:], bsrow[:]))
        S.op("dve", [("bs_hi",), ("abp",)], [("bs_lo",)],
             lambda e: e.tensor_tensor(out=bs_lo[:], in0=bsrow[:], in1=bs_hi[:], op=ALU.subtract))

    def ab_layer(i, layer, s_in, s_out):
        off = 0
        sq, off = wview(off, [4, 256], BF16)
        rt, off = wview(off, [256], F32)
        rstd = rt
        fbuf, off = wview(off, [4, NSEQ * 19], F32)
        a_ext, _ = wview(off, [4, 15 + 256], F32)
        as_ext, off = wview(off, [4, NSEQ, 19], F32)
        pa, off = wview(off, [NSEQ * 19], F32)
        pb, off = wview(off, [NSEQ * 19], F32)
        pooled0, off = wview(off, [4, 256], BF16)
        pooled1, off = wview(off, [4, 256], BF16)
        pooleds = [pooled0, pooled1]
        u, off = wview(off, [4, 256], F32)
        vf, off = wview(off, [2, 512], F32)
        vnb, off = wview(off, [2, 512], BF16)
        st6, off = wview(off, [2, 6], F32)
        mv, off = wview(off, [2, 2], F32)
        vr, off = wview(off, [2, 2], F32)
        w_in = rview(s_in, 0, 8, 1536)
        pw = rview(s_in, 12288, 4, 128)
        ginx = PC_NMIX + layer * 8

        AEXT_KEYS = [("a_halo",)] + [("a_new", c) for c in range(4)]

        def load_sample_hist():
            with nc.allow_non_contiguous_dma(reason="small history rows"):
                S.dma("sp", "d_hist_ab", [], [("as_hist",)] + AEXT_KEYS, lambda e: [
                    e.dma_start(out=as_ext[:, c, :, 0:15], in_=spT[i, c * 128:(c + 1) * 128]) for c in range(4)])
        S.dma("sp", "d_old", [], [], lambda e: [e.dma_start(out=nps_old[i], in_=sp_tm[i, :, 4:15, :])])
        S.op("pool", [], [("a_halo",)], lambda e: e.memset(a_ext[:, :, 0:15], 0.0))

        nb = {}

        def norm_sq(tl, hf=0):
            n = tl.n
            S.op("act", tl.xk(), [("sq",)],
                 lambda e: e.activation(out=sq[:, :, :n], in_=X[:, hf * 4:hf * 4 + 4, tl.x0:tl.x0 + n], func=AF.Square))

        def norm_ss(tl, hf=0):
            n = tl.n
            if hf == 0:
                nb[tl.x0] = bank()
            b = nb[tl.x0]

            def mm(e):
                for c in range(4):
                    ins = e.matmul(psum[:, b, :n], lhsT=ones_b[:], rhs=sq[:, c, :n], start=(hf == 0 and c == 0),
                                   stop=(hf == 1 and c == 3), skip_group_check=True)
                return ins
            S.op("pe", [("sq",), ("ones",)] + ([("ps", b)] if hf == 1 else []), [("ps", b)], mm)

        def norm_sqrt(tl):
            n = tl.n
            b = nb[tl.x0]
            S.op("act", [("ps", b), ("eps",)], [("rt",)],
                 lambda e: e.activation(out=rt[:, :n], in_=psum[:, b, :n], func=AF.Sqrt, scale=1.0 / D,
                                        bias=eps_t[:, 0:1]))

        def norm_fin(tl):
            n = tl.n
            S.op("dve", [("rt",)], [("rt",)], lambda e: e.reciprocal(out=rstd[:, :n], in_=rt[:, :n]))
            norm_apply(tl, rstd, ginx, lambda c: A[:, c, tl.a0:tl.a0 + n], lambda c: tl.ak(c), rkey=("rt",))

        def stage_z(tl, pi, nxt_tl, sq0_done=False):
            n = tl.n
            pooled = pooleds[pi]
            hk = tl.ak() + slot_keys(s_in, 1)
            hkv = tl.ak() + slot_keys(s_in, 0)
            nsub = max(1, n // 128)
            m = min(128, n)
            if nxt_tl is not None:
                if not sq0_done:
                    norm_sq(nxt_tl, 0)
                norm_ss(nxt_tl, 0)
                norm_sq(nxt_tl, 1)
            for sbk in range(nsub):
                b = bank()

                def mm(e, sbk=sbk, b=b):
                    for k in range(8):
                        ins = e.matmul(psum[:m, b, :], lhsT=A[:, k, tl.a0 + sbk * 128:tl.a0 + sbk * 128 + m],
                                       rhs=w_in[:, k, 1024:1536], start=(k == 0), stop=(k == 7))
                    return ins
                S.op("pe", hkv, [("ps", b)], mm)
                S.op("act", [("ps", b)], [("vf", sbk)],
                     lambda e, b=b, sbk=sbk: e.activation(out=vf[:m, sbk, :], in_=psum[:m, b, :], func=AF.Gelu))
                S.op("dve", [("vf", sbk)], [("st6", sbk)], lambda e, sbk=sbk: e.bn_stats(st6[:m, sbk, :], vf[:m, sbk, :]))
                S.op("dve", [("st6", sbk)], [("mv", sbk)], lambda e, sbk=sbk: e.bn_aggr(mv[:m, sbk, :], st6[:m, sbk, :]))
            for c in range(4):
                b = bank()

                def mm(e, c=c, b=b):
                    for k in range(8):
                        ins = e.matmul(psum[:, b, :n], lhsT=w_in[:, k, c * 128:(c + 1) * 128],
                                       rhs=A[:, k, tl.a0:tl.a0 + n], start=(k == 0), stop=(k == 7))
                    return ins
                S.op("pe", hk, [("ps", b)], mm)
                if tl.sample:
                    S.op("act", [("ps", b)], [("as_new", c)],
                         lambda e, c=c, b=b: e.activation(out=as_ext[:, c, :, 15:19],
                                                          in_=psum[:, b, :n].rearrange("p (s j) -> p s j", j=4),
                                                          func=AF.Copy), after=AEXT_KEYS)
                else:
                    S.op("act", [("ps", b)], [("a_new", c)],
                         lambda e, c=c, b=b: e.activation(out=a_ext[:, c, 15:15 + n], in_=psum[:, b, :n], func=AF.Copy))
            deferred = []
            for g in range(4):
                if tl.sample:
                    src = as_ext[:, g]
                    L = 19

                    def sl(ap, lo, hi):
                        return ap[:, :, lo:hi]
                    bufs = [pa[:, 0:NSEQ * 19].rearrange("p (s l) -> p s l", l=19),
                            pb[:, 0:NSEQ * 19].rearrange("p (s l) -> p s l", l=19)]
                    fin = fbuf[:, g, 0:NSEQ * 19].rearrange("p (s l) -> p s l", l=19)
                    rk = [("as_hist",), ("as_new", g)]
                else:
                    src = a_ext[:, g]
                    L = 15 + n

                    def sl(ap, lo, hi):
                        return ap[:, lo:hi]
                    bufs = [pa, pb]
                    fin = fbuf[:, g, :]
                    rk = [("a_halo",), ("a_new", g)]
                cur = src
                ckey = None
                sh = 1
                for step in range(g + 1):
                    last = (step == g)
                    dst = fin if last else bufs[step % 2]
                    dkey = ("fbuf", g) if last else ("pbuf", step % 2)
                    lo = 2 * sh - 1
                    S.op("pool", rk if step == 0 else [ckey], [dkey],
                         lambda e, cur=cur, dst=dst, sh=sh, sl=sl, L=L, lo=lo: e.tensor_tensor(
                             out=sl(dst, lo, L), in0=sl(cur, lo, L), in1=sl(cur, lo - sh, L - sh), op=ALU.add))
                    cur, ckey = dst, dkey
                    sh *= 2
                w = 2 ** (g + 1)

                def finish(g=g, w=w, fin=fin, rk=rk):
                    fk = ("fbuf", g)
                    if tl.sample:
                        S.op("dve", [fk] + rk, [("pooled", pi, g)],
                             lambda e: e.scalar_tensor_tensor(
                                 out=pooled[:, g, :n].rearrange("p (s j) -> p s j", j=4), in0=fin[:, :, 15:19],
                                 scalar=1.0 / w, in1=as_ext[:, g, :, 15:19], op0=ALU.mult, op1=ALU.subtract))
                    else:
                        S.op("dve", [fk] + rk, [("pooled", pi, g)],
                             lambda e: e.scalar_tensor_tensor(
                                 out=pooled[:, g, :n], in0=fin[:, 15:15 + n], scalar=1.0 / w, in1=a_ext[:, g, 15:15 + n],
                                 op0=ALU.mult, op1=ALU.subtract))
                        if tl.x0 == 0:
                            S.op("dve", [fk, ("const",)], [fk],
                                 lambda e: e.tensor_tensor(out=fin[:, 15:31], in0=fin[:, 15:31], in1=invcnt[:, g, :],
                                                           op=ALU.mult))
                            S.op("dve", [fk] + rk + [("pooled", pi, g)], [("pooled", pi, g)],
                                 lambda e: e.tensor_tensor(out=pooled[:, g, 0:16], in0=fin[:, 15:31],
                                                           in1=a_ext[:, g, 15:31], op=ALU.subtract))
                deferred.append(finish)
            if tl.sample:
                S.dma("sp", "d_st", [("as_new", c) for c in range(4)], [], lambda e: [
                    e.dma_start(out=nps_new[i, c * 128:(c + 1) * 128, :].rearrange("p (s j) -> p s j", j=4),
                                in_=as_ext[:, c, :, 15:19]) for c in range(4)])
            else:
                if tl.x0 + n == SEQ:
                    with nc.allow_non_contiguous_dma(reason="small state rows"):
                        S.dma("sp", "d_st", [("a_new", c) for c in range(4)], [], lambda e: [
                            e.dma_start(out=npp[i].rearrange("(c p) r -> p c r", p=128), in_=a_ext[:, :, n:n + 15])])
                else:
                    S.op("pool", [("a_new", c) for c in range(4)], [("a_halo",)],
                         lambda e: e.tensor_copy(a_ext[:, :, 0:15], a_ext[:, :, n:n + 15]))
            if nxt_tl is not None:
                norm_ss(nxt_tl, 1)
            S.op("act", [("mv", s_) for s_ in range(nsub)] + [("eps",)], [("vr0",)],
                 lambda e: e.activation(out=vr[:m, 0, 0:nsub], in_=mv[:m, 0:nsub, 1], func=AF.Sqrt, bias=eps_t[:m, 0:1]))
            if nxt_tl is not None:
                norm_sqrt(nxt_tl)
            S.op("dve", [("vr0",)], [("vr1",)], lambda e: e.reciprocal(out=vr[:m, 1, 0:nsub], in_=vr[:m, 0, 0:nsub]))
            for sbk in range(nsub):
                S.op("dve", [("vf", sbk), ("mv", sbk), ("abp",)], [("vf", sbk)],
                     lambda e, sbk=sbk: e.scalar_tensor_tensor(out=vf[:m, sbk, :], in0=vf[:m, sbk, :],
                                                               scalar=mv[:m, sbk, 0:1], in1=lng_bc[:m, :],
                                                               op0=ALU.subtract, op1=ALU.mult))
                if tl.sample:
                    S.op("dve", [("vf", sbk), ("vr1",), ("abp",)], [("vf", sbk)],
                         lambda e, sbk=sbk: e.scalar_tensor_tensor(out=vf[:m, sbk, :], in0=vf[:m, sbk, :],
                                                                   scalar=vr[:m, 1, sbk:sbk + 1], in1=lnb_bc[:m, :],
                                                                   op0=ALU.mult, op1=ALU.add))
                    S.op("dve", [("vf", sbk)], [("vnb", sbk)],
                         lambda e, sbk=sbk: e.tensor_copy(vnb[:m, sbk, :], vf[:m, sbk, :]))
                    S.dma("sp", "d_gv", [("vf", sbk)], [], lambda e, sbk=sbk: [e.dma_start(out=gv[i], in_=vf[:m, sbk, :])])
                else:
                    S.op("dve", [("vf", sbk), ("vr1",), ("abp",)], [("vnb", sbk)],
                         lambda e, sbk=sbk: e.scalar_tensor_tensor(out=vnb[:m, sbk, :], in0=vf[:m, sbk, :],
                                                                   scalar=vr[:m, 1, sbk:sbk + 1], in1=lnb_bc[:m, :],
                                                                   op0=ALU.mult, op1=ALU.add))
            if nxt_tl is not None:
                norm_fin(nxt_tl)
            for c in range(4):
                b = bank()

                def mm(e, c=c, b=b):
                    for k in range(8):
                        ins = e.matmul(psum[:, b, :n], lhsT=w_in[:, k, 512 + c * 128:512 + (c + 1) * 128],
                                       rhs=A[:, k, tl.a0:tl.a0 + n], start=(k == 0), stop=(k == 7))
                    return ins
                S.op("pe", hk, [("ps", b)], mm)
                S.op("act", [("ps", b)], [("u", c)],
                     lambda e, c=c, b=b: e.activation(out=u[:, c, :n], in_=psum[:, b, :n], func=AF.Gelu))
            return deferred

        def stage_pm(tl, pi):
            n = tl.n
            pooled = pooleds[pi]
            for g in range(4):
                b = bank()
                S.op("pe", [("pooled", pi, g)] + slot_keys(s_in, 2), [("ps", b)],
                     lambda e, g=g, b=b: e.matmul(psum[:, b, :n], lhsT=pw[:, g, :], rhs=pooled[:, g, :n],
                                                  start=True, stop=True))
                S.op("act", [("ps", b), ("const",)], tl.ak(g),
                     lambda e, g=g, b=b: e.activation(out=A[:, g, tl.a0:tl.a0 + n], in_=psum[:, b, :n], func=AF.Identity,
                                                      scale=pcols[:, PC_PSCALE + i * 4 + g:PC_PSCALE + i * 4 + g + 1]))

        def stage_gate(tl):
            n = tl.n
            nsub = max(1, n // 128)
            for hd in range(4):
                b = bank()

                def mm(e, hd=hd, b=b):
                    if tl.sample:
                        o3 = psum[:, b, :n].rearrange("p (s j) -> p s j", j=4)
                        e.matmul(o3, lhsT=ones_b[0:1, :], rhs=bs_hi[0:1, hd, 0:4].unsqueeze(1).to_broadcast([1, NSEQ, 4]),
                                 start=True, stop=False, skip_group_check=True)
                        e.matmul(o3, lhsT=ones_b[0:1, :], rhs=bs_lo[0:1, hd, 0:4].unsqueeze(1).to_broadcast([1, NSEQ, 4]),
                                 start=False, stop=False, skip_group_check=True)
                        ins = e.matmul(psum[:, b, :n], lhsT=vnb[:n, 0, hd * 128:(hd + 1) * 128], rhs=wmTs[:, hd, :],
                                       start=False, stop=True, skip_group_check=True)
                    else:
                        first = True
                        for sbk in range(nsub):
                            cols = psum[:, b, sbk * 128:(sbk + 1) * 128]
                            e.matmul(cols, lhsT=ones_b[0:1, :], rhs=bs_hi[0:1, hd, :], start=first, stop=False,
                                     skip_group_check=True)
                            first = False
                            e.matmul(cols, lhsT=ones_b[0:1, :], rhs=bs_lo[0:1, hd, :], start=False, stop=False,
                                     skip_group_check=True)
                            ins = e.matmul(cols, lhsT=vnb[:, sbk, hd * 128:(hd + 1) * 128], rhs=wmT[:, hd, :],
                                           start=False, stop=(sbk == nsub - 1), skip_group_check=True)
                    return ins
                S.op("pe", [("vnb", s_) for s_ in range(nsub)] + [("wmT",), ("wmTs",), ("bs_hi",), ("bs_lo",), ("ones",)],
                     [("ps", b)], mm)
                S.op("dve", [("ps", b), ("u", hd)], tl.ak(4 + hd),
                     lambda e, hd=hd, b=b: e.tensor_tensor(out=A[:, 4 + hd, tl.a0:tl.a0 + n], in0=psum[:, b, :n],
                                                           in1=u[:, hd, :n], op=ALU.mult))

        subs = XT256
        norm_sq(subs[0], 0)
        norm_ss(subs[0], 0)
        norm_sq(subs[0], 1)
        norm_ss(subs[0], 1)
        norm_sqrt(subs[0])
        norm_fin(subs[0])
        prev = None
        ppend = []
        for idx, tl in enumerate(subs):
            if tl.sample:
                load_sample_hist()
            deferred = stage_z(tl, idx % 2, subs[idx + 1] if idx + 1 < len(subs) else None, sq0_done=(idx > 0))
            while ppend:
                proj_out(ppend.pop(0), s_out, 0)
            stage_gate(tl)
            for fn_ in deferred:
                fn_()
            if idx + 2 < len(subs):
                norm_sq(subs[idx + 2], 0)
            if prev is not None:
                stage_pm(prev, (idx - 1) % 2)
                if prev.x0 % 512 == 256:
                    ppend.append(XT512[prev.x0 // 512])
            prev = tl
        stage_pm(prev, (len(subs) - 1) % 2)
        while ppend:
            proj_out(ppend.pop(0), s_out, 0)
        proj_out(XT512[4], s_out, 0)

    def c_layer(i, layer, s_in, load_pw2):
        ginx = PC_NMIX + layer * 8
        w1 = rview(s_in, 0, 8, 2048)
        other = 1 - s_in
        off = 0
        gs_ext, off = wview(off, [8, NSEQ, 34], BF16)
        sq, off = wview(off, [8, 256], BF16)
        rt, off = wview(off, [256], F32)
        rstd, off = wview(off, [256], F32)
        h0, off = wview(off, [8, 256], BF16)
        h1, off = wview(off, [8, 256], BF16)
        hbuf = [h0, h1]
        sg0, off = wview(off, [256], F32)
        sg1, off = wview(off, [256], F32)
        gl0, off = wview(off, [256], F32)
        gl1, off = wview(off, [256], F32)
        sgs, gls = [sg0, sg1], [gl0, gl1]
        dg = {0: rview(other, 0, 4 * CW, 128), 1: rview(s_in, 0, 4 * CW, 128)}

        def build_diag_chunk(c):
            half, cc = c // 4, c % 4
            sl_ = other if half == 0 else s_in
            wcol = pcols[:, PC_WDW + (i * 8 + c) * CW:PC_WDW + (i * 8 + c + 1) * CW]
            S.op("dve", [("const",), ("identb",)], [("diag", c)],
                 lambda e: e.tensor_tensor(out=dg[half][:, cc * CW:(cc + 1) * CW, :],
                                           in0=ident_b[:].unsqueeze(1).to_broadcast([128, CW, 128]),
                                           in1=wcol.unsqueeze(2).to_broadcast([128, CW, 128]), op=ALU.mult),
                 after=slot_keys(sl_))

        with nc.allow_non_contiguous_dma(reason="small history rows"):
            S.dma("pool", "d_hist_c", [], [("gs_hist",)], lambda e: [
                e.dma_start(out=gs_ext[:, c, :, 0:30], in_=scT[i, c * 128:(c + 1) * 128]) for c in range(8)])
        S.dma("sp", "d_old", [], [], lambda e: [e.dma_start(out=ncs_old[i], in_=sc_tm[i, :, 4:30, :])])

        def c_norm_sq(idx):
            tl = XT256[idx]
            S.op("act", tl.xk(), [("sq",)],
                 lambda e: e.activation(out=sq[:, :, :tl.n], in_=X[:, :, tl.x0:tl.x0 + tl.n], func=AF.Square))

        def c_norm_rest(idx):
            tl = XT256[idx]
            n_ = tl.n
            hh = hbuf[idx % 2]
            b = bank()

            def mm(e):
                for c in range(8):
                    ins = e.matmul(psum[:, b, :n_], lhsT=ones_b[:], rhs=sq[:, c, :n_], start=(c == 0), stop=(c == 7))
                return ins
            S.op("pe", [("sq",), ("ones",)], [("ps", b)], mm)
            S.op("act", [("ps", b), ("eps",)], [("rt",)],
                 lambda e: e.activation(out=rt[:, :n_], in_=psum[:, b, :n_], func=AF.Sqrt, scale=1.0 / D,
                                        bias=eps_t[:, 0:1]))
            S.op("dve", [("rt",)], [("rstd",)], lambda e: e.reciprocal(out=rstd[:, :n_], in_=rt[:, :n_]))

        def c_norm_app(idx, chunks):
            tl = XT256[idx]
            hh = hbuf[idx % 2]
            norm_apply(tl, rstd, ginx, lambda c: hh[:, c, :tl.n], lambda c: [("h", idx % 2, c)], chunks=chunks)

        c_norm_sq(0)
        c_norm_rest(0)
        c_norm_app(0, range(8))
        c_norm_sq(1)
        for idx, tl in enumerate(XT256):
            n = tl.n
            if 1 <= idx <= 4:
                build_diag_chunk(idx - 1)
            h = hbuf[idx % 2]
            hk = [("h", idx % 2, c) for c in range(8)] + slot_keys(s_in)
            for c in range(8):
                if idx + 1 < len(XT256):
                    if c == 0:
                        c_norm_rest(idx + 1)
                    elif 1 <= c <= 4:
                        c_norm_app(idx + 1, range(2 * (c - 1), 2 * (c - 1) + 2))
                    elif c == 5 and idx + 2 < len(XT256):
                        c_norm_sq(idx + 2)
                b1, b2 = bank(), bank()

                def mm(e, c=c, b1=b1, b2=b2):
                    for k in range(8):
                        e.matmul(psum[:, b1, :n], lhsT=w1[:, k, c * 128:(c + 1) * 128], rhs=h[:, k, :n],
                                 start=(k == 0), stop=(k == 7))
                    for k in range(8):
                        ins = e.matmul(psum[:, b2, :n], lhsT=w1[:, k, 1024 + c * 128:1024 + (c + 1) * 128],
                                       rhs=h[:, k, :n], start=(k == 0), stop=(k == 7))
                    return ins
                S.op("pe", hk, [("ps", b1), ("ps", b2)], mm)
                sg, gl = sgs[c % 2], gls[c % 2]
                S.op("act", [("ps", b2)], [("sg", c % 2)],
                     lambda e, b2=b2, sg=sg: e.activation(out=sg[:, :n], in_=psum[:, b2, :n], func=AF.Sigmoid))
                S.op("dve", [("ps", b1), ("sg", c % 2)], [("gl", c % 2)],
                     lambda e, b1=b1, sg=sg, gl=gl: e.tensor_tensor(out=gl[:, :n], in0=psum[:, b1, :n], in1=sg[:, :n],
                                                                    op=ALU.mult))
                if tl.sample:
                    S.op("act", [("gl", c % 2)], [("gs_new", c)],
                         lambda e, c=c, gl=gl: e.activation(out=gs_ext[:, c, :, 30:34],
                                                            in_=gl[:, :n].rearrange("p (s j) -> p s j", j=4),
                                                            func=AF.Copy))
                    S.dma("sp", "d_sg%d" % (c % 2), [("gl", c % 2)], [],
                          lambda e, c=c, gl=gl: [e.dma_start(out=ncs_new[i, c * 128:(c + 1) * 128, :], in_=gl[:, :n])])
                else:
                    S.op("act", [("gl", c % 2)], tl.ak(c),
                         lambda e, c=c, gl=gl: e.activation(out=A[:, c, tl.a0:tl.a0 + n], in_=gl[:, :n], func=AF.Copy))
                    if tl.x0 + n == SEQ:
                        with nc.allow_non_contiguous_dma(reason="small state rows"):
                            S.dma("sp", "d_sg%d" % (c % 2), [("gl", c % 2)], [],
                                  lambda e, c=c, gl=gl: [e.dma_start(out=ncp[i, c * 128:(c + 1) * 128, :],
                                                                     in_=gl[:, n - 30:n])])
        for c in range(4, 8):
            build_diag_chunk(c)
        snap = S.fence_snapshot()
        S.fence_apply(snap)

        off = 8 * NSEQ * 34 * 2
        cv, off = wview(off, [8, 256], F32)
        cvb, off = wview(off, [8, 256], BF16)
        sqb, off = wview(off, [8, 256], BF16)
        mean, off = wview(off, [256], F32)
        var, off = wview(off, [256], F32)
        rs2, off = wview(off, [256], F32)
        order = [XT256[s] for s in range(7, -1, -1)] + [XT256[8]]
        pending = []
        for tl in order:
            n = tl.n
            if tl.sample:
                gkeys = [("gs_hist",)] + [("gs_new", c) for c in range(8)]
            else:
                s_ = tl.x0 // 256
                gkeys = tl.ak() + ([("A", s_ - 1, c) for c in range(8)] if s_ > 0 else [("A", "pad", c) for c in range(8)])
            for c in range(8):
                b = bank()
                half, cc = c // 4, c % 4

                def mm(e, c=c, b=b, half=half, cc=cc):
                    for k in range(CW):
                        if tl.sample:
                            rhs = gs_ext[:, c, :, k:k + 4]
                            o = psum[:, b, :n].rearrange("p (s j) -> p s j", j=4)
                        else:
                            rhs = A[:, c, tl.x0 + k:tl.x0 + k + n]
                            o = psum[:, b, :n]
                        ins = e.matmul(o, lhsT=dg[half][:, cc * CW + k, :], rhs=rhs, start=(k == 0), stop=(k == CW - 1))
                    return ins
                S.op("pe", gkeys + [("diag", c)] + slot_keys(0) + slot_keys(1), [("ps", b)], mm)
                if c == 0 and pending:
                    pending.pop()()
                S.op("act", [("ps", b), ("const",)], [("cv", c)],
                     lambda e, c=c, b=b: e.activation(out=cv[:, c, :n], in_=psum[:, b, :n], func=AF.Identity,
                                                      bias=pcols[:, PC_BDW + i * 8 + c:PC_BDW + i * 8 + c + 1]))
                S.op("act", [("cv", c)], [("sqb", c)],
                     lambda e, c=c: e.activation(out=sqb[:, c, :n], in_=cv[:, c, :n], func=AF.Square))
                S.op("act", [("cv", c)], [("cvb", c)],
                     lambda e, c=c: e.activation(out=cvb[:, c, :n], in_=cv[:, c, :n], func=AF.Copy))
            def post(tl=tl, n=n):
              b1, b2 = bank(), bank()

              def mm2(e, b1=b1, b2=b2):
                  for c in range(8):
                      e.matmul(psum[:, b1, :n], lhsT=ones_b[:], rhs=cvb[:, c, :n], start=(c == 0), stop=(c == 7))
                  for c in range(8):
                      ins = e.matmul(psum[:, b2, :n], lhsT=ones_b[:], rhs=sqb[:, c, :n], start=(c == 0), stop=(c == 7))
                  return ins
              S.op("pe", [("cvb", c) for c in range(8)] + [("sqb", c) for c in range(8)] + [("ones",)],
                   [("ps", b1), ("ps", b2)], mm2)
              S.op("dve", [("ps", b1)], [("mean",)],
                   lambda e, b1=b1: e.tensor_scalar(out=mean[:, :n], in0=psum[:, b1, :n], scalar1=1.0 / D, scalar2=None,
                                                    op0=ALU.mult))
              S.op("dve", [("mean",)], [("var",)],
                   lambda e: e.tensor_tensor(out=var[:, :n], in0=mean[:, :n], in1=mean[:, :n], op=ALU.mult))
              S.op("dve", [("ps", b2), ("var",)], [("var",)],
                   lambda e, b2=b2: e.scalar_tensor_tensor(out=var[:, :n], in0=psum[:, b2, :n], scalar=1.0 / D,
                                                           in1=var[:, :n], op0=ALU.mult, op1=ALU.subtract))
              S.op("act", [("var",), ("eps",)], [("rs2",)],
                   lambda e: e.activation(out=rs2[:, :n], in_=var[:, :n], func=AF.Sqrt, bias=eps_t[:, 0:1]))
              S.op("dve", [("rs2",)], [("rs2",)], lambda e: e.reciprocal(out=rs2[:, :n], in_=rs2[:, :n]))
              cvk = [("cv", c) for c in range(8)]
              S.op("dve", cvk + [("mean",)], cvk,
                   lambda e: e.tensor_tensor(out=cv[:, :, :n], in0=cv[:, :, :n],
                                             in1=mean[:, :n].unsqueeze(1).to_broadcast([128, 8, n]), op=ALU.subtract))
              S.op("dve", cvk + [("rs2",)], cvk,
                   lambda e: e.tensor_tensor(out=cv[:, :, :n], in0=cv[:, :, :n],
                                             in1=rs2[:, :n].unsqueeze(1).to_broadcast([128, 8, n]), op=ALU.mult))
              for c in range(8):
                  S.op("act", [("cv", c), ("const",)], tl.ak(c),
                       lambda e, c=c: e.activation(out=A[:, c, tl.a0:tl.a0 + n], in_=cv[:, c, :n], func=AF.Silu,
                                                   scale=pcols[:, PC_LNG + i * 8 + c:PC_LNG + i * 8 + c + 1],
                                                   bias=pcols[:, PC_LNB + i * 8 + c:PC_LNB + i * 8 + c + 1]))
            pending.append(post)
        while pending:
            pending.pop()()
        s_out = load_pw2()
        for tl in [XT512[3], XT512[2], XT512[1], XT512[0], XT512[4]]:
            proj_out(tl, s_out, 0)
        return s_out


    def unit_ab_in(i):
        wv = w_in_d[i].rearrange("(kc p) n -> p kc n", p=128)
        return [(0, (8, 1536), wv[:, :, 1024:1536], (1024, 1536)),
                (0, (8, 1536), wv[:, :, 0:1024], (0, 1024)),
                (12288, (4, 128), pool_w_d[i].rearrange("g c d -> c g d"))]

    def unit_w(dram):
        return [(0, (8, D), dram.rearrange("(kc p) n -> p kc n", p=128))]

    work_fence = [None]

    def mixer_unit(layer):
        i = layer // 2
        if layer % 2 == 0:
            return load_unit(unit_ab_in(i))
        return load_unit([(0, (8, 2048), w_pw1_d[i].rearrange("(kc p) n -> p kc n", p=128))])

    yv = yT.rearrange("(c p) t -> p c t", p=128)
    ydma = [0]

    def final_norm_tile(tl, sq, rt, rstd):
        subs = [tl] if tl.sample else [XT256[tl.x0 // 256], XT256[tl.x0 // 256 + 1]]
        for st in subs:
            n = st.n
            norm_stats(st, sq, rt, rstd)
            for c in range(8):
                S.op("dve", st.xk(c) + [("rstd",), ("const",)], st.xk(c),
                     lambda e, c=c, st=st, n=n: e.scalar_tensor_tensor(
                         out=X[:, c, st.x0:st.x0 + n], in0=X[:, c, st.x0:st.x0 + n],
                         scalar=pcols[:, PC_NFIN + c:PC_NFIN + c + 1], in1=rstd[:, :n], op0=ALU.mult, op1=ALU.mult))
            ydma[0] += 1
            S.dma("sp", "d_y%d" % (ydma[0] % 4), st.xk(), [],
                  lambda e, st=st, n=n: [e.dma_start(out=yv[:, :, st.x0:st.x0 + n], in_=X[:, :, st.x0:st.x0 + n])])

    nxt = {}
    nxt[0] = mixer_unit(0)
    for layer in range(depth_run):
        i = layer // 2
        if work_fence[0] is not None:
            S.fence_apply(work_fence[0])
        if layer % 2 == 0:
            ab_params(i)
            s_in = nxt[layer]
            s_out = load_unit(unit_w(w_out_d[i]))
            ab_layer(i, layer, s_in, s_out)
        else:
            s_in = nxt[layer]
            c_layer(i, layer, s_in, lambda i=i: load_unit(unit_w(w_pw2_d[i])))
        work_fence[0] = S.fence_snapshot()
        pre = ffn_preload(layer)
        S.fence_apply(work_fence[0])

        def next_loader(layer=layer):
            if layer + 1 < depth_run:
                nxt[layer + 1] = mixer_unit(layer + 1)
        ffn_layer(layer, next_loader, final_norm_tile if layer == depth_run - 1 else None, pre_slots=pre)
        work_fence[0] = S.fence_snapshot()

    S.final_wait("sp")
    return nc


_PROGRAM = {}


def _consts():
    s = np.arange(128)
    maskT = (s[:, None] <= s[None, :]).astype(np.float32)
    q = np.arange(64)
    bd = ((q[:, None] // 4 == q[None, :] // 4) & (q[:, None] % 4 <= q[None, :] % 4)).astype(np.float32)
    inv = np.zeros((128, 4, 16), np.float32)
    for g, w in enumerate((2, 4, 8, 16)):
        inv[:, g, :] = 1.0 / np.minimum(np.arange(16) + 1, w)
    return maskT, bd, inv


def _col(v):
    v = np.asarray(v, np.float32)
    lead = v.shape[:-1]
    C = v.shape[-1] // 128
    return np.moveaxis(v.reshape(*lead, C, 128), -1, 0).reshape(128, -1)


def kernel(x_prompt, x_sample, state_pool, state_conv, norm_mix, norm_ffn, norm_final,
           ab_w_in, ab_pool_w, ab_pool_scale, ab_ln_g, ab_ln_b, ab_ws, ab_bs, ab_w_out,
           c_w_pw1, c_w_dw, c_b_dw, c_ln_g, c_ln_b, c_w_pw2, ffn_w1, ffn_w2, _depth_run=DEPTH):
    f = lambda a: np.ascontiguousarray(np.asarray(a, dtype=np.float32))
    x_prompt, x_sample, state_pool, state_conv = f(x_prompt), f(x_sample), f(state_pool), f(state_conv)
    maskT, bd, inv = _consts()
    wdwT = np.transpose(f(c_w_dw), (0, 2, 1))
    wdw_cols = wdwT.reshape(2, 8, 128, CW).transpose(2, 0, 1, 3).reshape(128, -1)
    pcols = np.concatenate([_col(f(norm_mix)), _col(f(norm_ffn)), _col(f(norm_final)), _col(f(ab_pool_scale)),
                            _col(f(c_b_dw)), _col(f(c_ln_g)), _col(f(c_ln_b)), wdw_cols], axis=1)
    assert pcols.shape == (128, PC_N), pcols.shape
    wsT = np.ascontiguousarray(np.transpose(f(ab_ws), (0, 1, 3, 2)))
    wsTs = np.ascontiguousarray(np.tile(wsT[:, :, :4, :4], (1, 1, 16, 16)))
    shared = {
        "pcols": np.ascontiguousarray(pcols), "ab_ln_g": f(ab_ln_g), "ab_ln_b": f(ab_ln_b), "wsT": wsT, "wsTs": wsTs,
        "ab_bs": f(ab_bs), "ident": np.eye(128, dtype=np.float32), "maskT": maskT, "bdmask": bd, "invcnt": inv,
        "ab_w_in": f(ab_w_in), "ab_pool_w": f(ab_pool_w), "ab_w_out": f(ab_w_out), "c_w_pw1": f(c_w_pw1),
        "c_w_pw2": f(c_w_pw2), "ffn_w1": f(ffn_w1), "ffn_w2": f(ffn_w2),
    }
    in_maps = []
    for c in range(NCORES):
        xs = x_sample[c * NSEQ:(c + 1) * NSEQ].reshape(NS, D)
        xT = np.ascontiguousarray(np.concatenate([x_prompt[c], xs], axis=0).T)
        sp = state_pool[:, c * NSEQ:(c + 1) * NSEQ]
        sc = state_conv[:, c * NSEQ:(c + 1) * NSEQ]
        m = dict(shared)
        m["xT"] = xT
        m["sp_tm"] = np.ascontiguousarray(sp)
        m["spT"] = np.ascontiguousarray(np.transpose(sp, (0, 3, 1, 2)))
        m["sc_tm"] = np.ascontiguousarray(sc)
        m["scT"] = np.ascontiguousarray(np.transpose(sc, (0, 3, 1, 2)))
        in_maps.append(m)
    if _depth_run not in _PROGRAM:
        _PROGRAM[_depth_run] = build_program(_depth_run)
    nc = _PROGRAM[_depth_run]
    res = run_bass_kernel_spmd(nc, in_maps, core_ids=list(range(NCORES)))
    R = res.results
    y_prompt = np.stack([R[c]["yT"][:, :SEQ].T for c in range(NCORES)])
    y_sample = np.concatenate([R[c]["yT"][:, SEQ:].T.reshape(NSEQ, 4, D) for c in range(NCORES)])
    npp = np.stack([np.transpose(R[c]["npp"], (0, 2, 1)) for c in range(NCORES)], axis=1)
    ncp = np.stack([np.transpose(R[c]["ncp"], (0, 2, 1)) for c in range(NCORES)], axis=1)

    def smp(old, new, width):
        outs = []
        for c in range(NCORES):
            o = R[c][old]
            nw = np.transpose(R[c][new], (0, 2, 1)).reshape(2, NSEQ, 4, width)
            outs.append(np.concatenate([o, nw], axis=2))
        return np.concatenate(outs, axis=1)
    nps = smp("nps_old", "nps_new", 512)
    ncs = smp("ncs_old", "ncs_new", D)
    gvs = np.concatenate([R[c]["gv"].reshape(2, NSEQ, 4, 512) for c in range(NCORES)], axis=1)
    return (np.ascontiguousarray(y_prompt), np.ascontiguousarray(y_sample), np.ascontiguousarray(npp),
            np.ascontiguousarray(nps), np.ascontiguousarray(ncp), np.ascontiguousarray(ncs),
            np.ascontiguousarray(gvs))
```

```python
import numpy as np
import concourse.bass as bass
import concourse.mybir as mybir
from concourse.bass_utils import run_bass_kernel_spmd

F32 = mybir.dt.float32
BF16 = mybir.dt.bfloat16
AF = mybir.ActivationFunctionType
ALU = mybir.AluOpType

NCORES = 8
D = 1024
SEQ = 2048
NS = 64
NSEQ = 16
NTOK = SEQ + NS
DEPTH = 4
EPS = 1e-6
APAD = 30
AW = APAD + SEQ + NS
CW = 31
SLOT = 16384

PC_NMIX = 0
PC_NFFN = 32
PC_NFIN = 64
PC_PSCALE = 72
PC_BDW = 80
PC_LNG = 96
PC_LNB = 112
PC_WDW = 128
PC_N = 128 + 2 * 8 * 31


class Sched:
    def __init__(self, nc):
        self.nc = nc
        self.eng = dict(pe=nc.tensor, act=nc.scalar, dve=nc.vector, pool=nc.gpsimd, sp=nc.sync)
        self.sems = {}
        self.cnt = {}
        self.seen = {e: {} for e in self.eng}
        self.lastw = {}
        self.readers = {}
        for e in ("pe", "act", "dve", "pool"):
            self._sem(e)

    def _sem(self, src):
        if src not in self.sems:
            self.sems[src] = self.nc.alloc_semaphore("s_" + src)
            self.cnt[src] = 0
        return self.sems[src]

    def _deps(self, reads, writes):
        need = {}

        def add(ev):
            if ev is not None:
                s, c = ev
                if need.get(s, 0) < c:
                    need[s] = c
        for r in reads:
            add(self.lastw.get(r))
        for r in writes:
            add(self.lastw.get(r))
            for s, c in self.readers.get(r, {}).items():
                add((s, c))
        return need

    def _wait(self, e, need):
        for s, c in need.items():
            if e == "pe" and s == "pe":
                continue
            if self.seen[e].get(s, 0) >= c:
                continue
            self.eng[e].wait_ge(self.sems[s], c)
            self.seen[e][s] = c

    def _commit(self, src, c, reads, writes):
        for r in reads:
            d = self.readers.setdefault(r, {})
            if d.get(src, 0) < c:
                d[src] = c
        for r in writes:
            self.lastw[r] = (src, c)
            self.readers[r] = {}

    def op(self, e, reads, writes, fn, after=()):
        self._wait(e, self._deps(reads, list(writes) + list(after)))
        ins = fn(self.eng[e])
        self.cnt[e] += 1
        ins.then_inc(self.sems[e], 1)
        self._commit(e, self.cnt[e], reads, writes)

    def dma(self, q, stream, reads, writes, fn, after=()):
        self._sem(stream)
        self._wait(q, self._deps(reads, list(writes) + list(after)))
        for ins in fn(self.eng[q]):
            self.cnt[stream] += 16
            ins.then_inc(self.sems[stream], 16)
        self._commit(stream, self.cnt[stream], reads, writes)

    def fence_snapshot(self):
        return {s: c for s, c in self.cnt.items()}

    def fence_apply(self, snap):
        for e in ("act", "dve", "pool", "sp"):
            self._wait(e, {s: c for s, c in snap.items() if c > 0})

    def final_wait(self, e="sp"):
        self._wait(e, {s: c for s, c in self.cnt.items() if c > 0})


class Tl:
    def __init__(self, x0, n, sample=False):
        self.x0, self.n, self.sample = x0, n, sample
        self.a0 = APAD + x0
        self.blocks = ["s"] if sample else list(range(x0 // 256, (x0 + n) // 256))

    def xk(self, c=None):
        cs = range(8) if c is None else [c]
        return [("X", b, cc) for b in self.blocks for cc in cs]

    def ak(self, c=None):
        cs = range(8) if c is None else [c]
        return [("A", b, cc) for b in self.blocks for cc in cs]


def build_program(depth_run=DEPTH):
    nc = bass.Bass("TRN2", target_bir_lowering=False)

    def din(name, shape):
        return nc.dram_tensor(name, list(shape), F32, kind="ExternalInput").ap()

    def dout(name, shape):
        return nc.dram_tensor(name, list(shape), F32, kind="ExternalOutput").ap()

    xT = din("xT", [D, NTOK])
    spT = din("spT", [2, 512, NSEQ, 15])
    sp_tm = din("sp_tm", [2, NSEQ, 15, 512])
    scT = din("scT", [2, D, NSEQ, 30])
    sc_tm = din("sc_tm", [2, NSEQ, 30, D])
    pcols_d = din("pcols", [128, PC_N])
    lng_d = din("ab_ln_g", [2, 512])
    lnb_d = din("ab_ln_b", [2, 512])
    wsT_d = din("wsT", [2, 4, 128, 128])
    wsTs_d = din("wsTs", [2, 4, 64, 64])
    bs_d = din("ab_bs", [2, 4, 128])
    ident_d = din("ident", [128, 128])
    maskT_d = din("maskT", [128, 128])
    bdmask_d = din("bdmask", [64, 64])
    invcnt_d = din("invcnt", [128, 4, 16])
    w_in_d = din("ab_w_in", [2, D, 1536])
    pool_w_d = din("ab_pool_w", [2, 4, 128, 128])
    w_out_d = din("ab_w_out", [2, D, D])
    w_pw1_d = din("c_w_pw1", [2, D, 2 * D])
    w_pw2_d = din("c_w_pw2", [2, D, D])
    w1_d = din("ffn_w1", [DEPTH, D, 4 * D])
    w2_d = din("ffn_w2", [DEPTH, 4 * D, D])

    yT = dout("yT", [D, NTOK])
    npp = dout("npp", [2, 512, 15])
    nps_old = dout("nps_old", [2, NSEQ, 11, 512])
    nps_new = dout("nps_new", [2, 512, NS])
    ncp = dout("ncp", [2, D, 30])
    ncs_old = dout("ncs_old", [2, NSEQ, 26, D])
    ncs_new = dout("ncs_new", [2, D, NS])
    gv = dout("gv", [2, NS, 512])

    S = Sched(nc)

    def sb(name, shape, dt=F32):
        return nc.alloc_sbuf_tensor(name, list(shape), dt)

    X = sb("X", [128, 8, NTOK])
    A = sb("A", [128, 8, AW], BF16)
    ring = sb("ring", [128, 2, SLOT], BF16)
    pcols = sb("pcols_sb", [128, PC_N])
    ones_b = sb("ones_b", [128, 128], BF16)
    ident_b = sb("ident_b", [128, 128], BF16)
    eps_t = sb("eps_t", [128, 1])
    maskT = sb("maskT_sb", [128, 128])
    bdmask = sb("bdmask_sb", [64, 64])
    invcnt = sb("invcnt_sb", [128, 4, 16])
    lng_bc = sb("lng_bc", [128, 512])
    lnb_bc = sb("lnb_bc", [128, 512])
    wmT = sb("wmT", [128, 4, 128], BF16)
    wmTs = sb("wmTs", [64, 4, 64], BF16)
    bsrow = sb("bsrow", [1, 4, 128])
    bs_hi = sb("bs_hi", [1, 4, 128], BF16)
    bs_lo = sb("bs_lo", [1, 4, 128], BF16)
    WORKB = 31488
    work = sb("work", [128, WORKB], mybir.dt.uint8)
    psum = nc.alloc_psum_tensor("psum", [128, 8, 512], F32)

    def wview(off, shape, dt):
        esz = 4 if dt == F32 else 2
        n = int(np.prod(shape))
        assert off % 4 == 0
        ap = work[:, off:off + n * esz].bitcast(dt)
        if len(shape) == 2:
            ap = ap.rearrange("p (a b) -> p a b", b=shape[1])
        elif len(shape) == 3:
            ap = ap.rearrange("p (a b c) -> p a b c", b=shape[1], c=shape[2])
        off2 = off + n * esz
        assert off2 <= WORKB, (off2, WORKB)
        return ap, off2

    bank_ctr = [0]

    def bank():
        b = bank_ctr[0] % 8
        bank_ctr[0] += 1
        return b

    def ld_consts(e):
        out = []
        out.append(e.dma_start(out=pcols[:], in_=pcols_d))
        out.append(e.dma_start(out=maskT[:], in_=maskT_d))
        out.append(e.dma_start(out=bdmask[:], in_=bdmask_d))
        out.append(e.dma_start(out=invcnt[:], in_=invcnt_d))
        return out
    S.dma("sp", "d_const", [], [("const",)], ld_consts)
    S.dma("pool", "d_ident", [], [("identb",)], lambda e: [e.dma_start(out=ident_b[:], in_=ident_d)])
    S.op("dve", [], [("ones",)], lambda e: e.memset(ones_b[:], 1.0))
    S.op("dve", [], [("eps",)], lambda e: e.memset(eps_t[:], EPS))
    S.op("pool", [], [("A", "pad", c) for c in range(8)], lambda e: e.memset(A[:, :, 0:APAD], 0.0))

    xv = xT.rearrange("(c p) t -> p c t", p=128)
    XT512 = [Tl(512 * t, 512) for t in range(4)] + [Tl(SEQ, NS, sample=True)]
    XT256 = [Tl(256 * s, 256) for s in range(8)] + [Tl(SEQ, NS, sample=True)]
    for i, tl in enumerate(XT256[:2] + XT512[1:]):
        S.dma("sp", "d_x%d" % i, [], tl.xk(),
              lambda e, tl=tl: [e.dma_start(out=X[:, :, tl.x0:tl.x0 + tl.n], in_=xv[:, :, tl.x0:tl.x0 + tl.n])])

    slot_ctr = [0]

    NPART = 3

    def slot_keys(s, part=None):
        return [("ring", s, j) for j in (range(NPART) if part is None else (part,))]

    def load_unit(parts):
        s = slot_ctr[0] % 2
        slot_ctr[0] += 1
        S._wait("pool", S._deps([], slot_keys(s)))
        for j in range(NPART):
            st = "d_w%d_%d" % (s, j)
            S._sem(st)
            if j < len(parts):
                off, (k, n), src = parts[j][:3]
                dst = ring[:, s, off:off + k * n].rearrange("p (k n) -> p k n", n=n)
                if len(parts[j]) > 3:
                    lo, hi = parts[j][3]
                    dst = dst[:, :, lo:hi]
                S.cnt[st] += 16
                nc.gpsimd.dma_start(out=dst, in_=src).then_inc(S.sems[st], 16)
                S._commit(st, S.cnt[st], [], [("ring", s, j)])
            else:
                st0 = "d_w%d_0" % s
                S._commit(st0, S.cnt[st0], [], [("ring", s, j)])
        return s

    def rview(s, off, k, n):
        return ring[:, s, off:off + k * n].rearrange("p (k n) -> p k n", n=n)

    def norm_stats(tl, sq, rt, rstd):
        n = tl.n
        S.op("act", tl.xk(), [("sq",)],
             lambda e: e.activation(out=sq[:, :, :n], in_=X[:, :, tl.x0:tl.x0 + n], func=AF.Square))
        b = bank()

        def mm(e):
            for c in range(8):
                ins = e.matmul(psum[:, b, :n], lhsT=ones_b[:], rhs=sq[:, c, :n], start=(c == 0), stop=(c == 7))
            return ins
        S.op("pe", [("sq",), ("ones",)], [("ps", b)], mm)
        S.op("act", [("ps", b), ("eps",)], [("rt",)],
             lambda e: e.activation(out=rt[:, :n], in_=psum[:, b, :n], func=AF.Sqrt, scale=1.0 / D, bias=eps_t[:, 0:1]))
        S.op("dve", [("rt",)], [("rstd",)], lambda e: e.reciprocal(out=rstd[:, :n], in_=rt[:, :n]))

    def norm_apply(tl, rstd, gidx, dst_fn, dst_keys_fn, rkey=("rstd",), chunks=range(8)):
        n = tl.n
        for c in chunks:
            S.op("dve", tl.xk(c) + [rkey, ("const",)], dst_keys_fn(c),
                 lambda e, c=c: e.scalar_tensor_tensor(out=dst_fn(c), in0=X[:, c, tl.x0:tl.x0 + n],
                                                       scalar=pcols[:, gidx + c:gidx + c + 1], in1=rstd[:, :n],
                                                       op0=ALU.mult, op1=ALU.mult))

    def proj_out(tl, slot, woff):
        n = tl.n
        w = rview(slot, woff, 8, D)
        for o in range(8):
            b = bank()

            def mm(e, o=o, b=b):
                for k in range(8):
                    ins = e.matmul(psum[:, b, :n], lhsT=w[:, k, o * 128:(o + 1) * 128], rhs=A[:, k, tl.a0:tl.a0 + n],
                                   start=(k == 0), stop=(k == 7))
                return ins
            S.op("pe", tl.ak() + slot_keys(slot), [("ps", b)], mm)
            S.op("dve", [("ps", b)] + tl.xk(o), tl.xk(o),
                 lambda e, o=o, b=b: e.tensor_tensor(out=X[:, o, tl.x0:tl.x0 + n], in0=psum[:, b, :n],
                                                     in1=X[:, o, tl.x0:tl.x0 + n], op=ALU.add))

    def ffn_units(layer):
        w1v = w1_d[layer].rearrange("(kc p) n -> p kc n", p=128)
        w2v = w2_d[layer].rearrange("(jc p) n -> p jc n", p=128)

        def unit(q):
            return [(0, (8, 1024), w1v[:, :, q * 1024:(q + 1) * 1024]),
                    (8192, (8, 1024), w2v[:, q * 8:(q + 1) * 8, :])]
        return unit

    def ffn_preload(layer):
        unit = ffn_units(layer)
        return [load_unit(unit(0)), load_unit(unit(1))]

    def ffn_layer(layer, next_unit_loader, after_last_O=None, pre_slots=None, tiles=None, first_norm_done=False):
        off = 0
        sq, off = wview(off, [8, 256], BF16)
        rt, off = wview(off, [256], F32)
        rstd, off = wview(off, [256], F32)
        hid0, off = wview(off, [8, 512], BF16)
        hid1, off = wview(off, [8, 512], BF16)
        rb0, off = wview(off, [512], F32)
        rb1, off = wview(off, [512], F32)
        hids = [hid0, hid1]
        rbs = [rb0, rb1]
        unit = ffn_units(layer)

        hctr = [0]

        def H(q, tl, slot, hooks=None):
            n = tl.n
            hb = hctr[0] % 2
            hctr[0] += 1
            w1 = rview(slot, 0, 8, 1024)
            for j in range(8):
                if hooks and j in hooks:
                    hooks[j]()
                b = bank()

                def mm(e, j=j, b=b):
                    for k in range(8):
                        ins = e.matmul(psum[:, b, :n], lhsT=w1[:, k, j * 128:(j + 1) * 128],
                                       rhs=A[:, k, tl.a0:tl.a0 + n], start=(k == 0), stop=(k == 7))
                    return ins
                S.op("pe", tl.ak() + slot_keys(slot, 0), [("ps", b)], mm)
                rb = rbs[j % 2]
                S.op("act", [("ps", b)], [("rb", j % 2)],
                     lambda e, b=b, rb=rb: e.activation(out=rb[:, :n], in_=psum[:, b, :n], func=AF.Relu))
                S.op("dve", [("ps", b), ("rb", j % 2)], [("hid", hb, j)],
                     lambda e, b=b, rb=rb, j=j: e.tensor_tensor(out=hids[hb][:, j, :n], in0=psum[:, b, :n],
                                                                in1=rb[:, :n], op=ALU.mult))
            return hb

        def O(tl, slot, hb):
            n = tl.n
            w2 = rview(slot, 8192, 8, 1024)
            for o in range(8):
                b = bank()

                def mm(e, o=o, b=b):
                    for j in range(8):
                        ins = e.matmul(psum[:, b, :n], lhsT=w2[:, j, o * 128:(o + 1) * 128], rhs=hids[hb][:, j, :n],
                                       start=(j == 0), stop=(j == 7))
                    return ins
                S.op("pe", [("hid", hb, j) for j in range(8)] + slot_keys(slot, 1), [("ps", b)], mm)
                S.op("dve", [("ps", b)] + tl.xk(o), tl.xk(o),
                     lambda e, o=o, b=b: e.tensor_tensor(out=X[:, o, tl.x0:tl.x0 + n], in0=psum[:, b, :n],
                                                         in1=X[:, o, tl.x0:tl.x0 + n], op=ALU.add))

        tiles = XT512 if tiles is None else tiles
        slots = [None] * 4
        if pre_slots is None:
            pre_slots = [load_unit(unit(0)), load_unit(unit(1))]
        slots[0], slots[1] = pre_slots
        pend = None
        fin_pend = []
        for q in range(4):
            if q + 1 < 4:
                pass
            for ti, tl in enumerate(tiles):
                hooks = None
                if q == 0:
                    def n_sq(st):
                        S.op("act", st.xk(), [("sq",)],
                             lambda e: e.activation(out=sq[:, :, :st.n], in_=X[:, :, st.x0:st.x0 + st.n], func=AF.Square))

                    def n_mid(st):
                        n_ = st.n
                        b = bank()

                        def mm(e):
                            for c in range(8):
                                ins = e.matmul(psum[:, b, :n_], lhsT=ones_b[:], rhs=sq[:, c, :n_], start=(c == 0),
                                               stop=(c == 7))
                            return ins
                        S.op("pe", [("sq",), ("ones",)], [("ps", b)], mm)
                        S.op("act", [("ps", b), ("eps",)], [("rt",)],
                             lambda e: e.activation(out=rt[:, :n_], in_=psum[:, b, :n_], func=AF.Sqrt, scale=1.0 / D,
                                                    bias=eps_t[:, 0:1]))
                        S.op("dve", [("rt",)], [("rstd",)], lambda e: e.reciprocal(out=rstd[:, :n_], in_=rt[:, :n_]))

                    def n_app(st, chunks):
                        norm_apply(st, rstd, PC_NFFN + layer * 8,
                                   lambda c: A[:, c, st.a0:st.a0 + st.n], lambda c: st.ak(c), chunks=chunks)

                    def subs_of(tn):
                        return [tn] if tn.sample else [XT256[tn.x0 // 256], XT256[tn.x0 // 256 + 1]]
                    if ti == 0 and not first_norm_done:
                        for st in subs_of(tiles[0]):
                            n_sq(st)
                            n_mid(st)
                            n_app(st, range(8))
                    if ti + 1 < len(tiles):
                        sb_ = subs_of(tiles[ti + 1])
                        two = len(sb_) > 1
                        n_sq(sb_[0])
                        hooks = {1: (lambda sb_=sb_: n_mid(sb_[0])),
                                 2: (lambda sb_=sb_: n_app(sb_[0], range(0, 4))),
                                 3: (lambda sb_=sb_, two=two: (n_app(sb_[0], range(4, 8)), n_sq(sb_[1]) if two else None)),
                                 5: (lambda sb_=sb_, two=two: n_mid(sb_[1]) if two else None),
                                 6: (lambda sb_=sb_, two=two: n_app(sb_[1], range(0, 4)) if two else None),
                                 7: (lambda sb_=sb_, two=two: n_app(sb_[1], range(4, 8)) if two else None)}
                hb = H(q, tl, slots[q], hooks)
                if pend is not None:
                    O(*pend[:3])
                    if pend[3] == 3 and after_last_O is not None:
                        if fin_pend:
                            after_last_O(fin_pend.pop(0), sq, rt, rstd)
                        fin_pend.append(pend[0])
                pend = (tl, slots[q], hb, q)
                if ti == 0 and q >= 1:
                    if q + 1 < 4:
                        slots[q + 1] = load_unit(unit(q + 1))
                    else:
                        next_unit_loader()
        O(*pend[:3])
        if after_last_O is not None:
            fin_pend.append(pend[0])
            while fin_pend:
                after_last_O(fin_pend.pop(0), sq, rt, rstd)

    def ab_params(i):
        S.dma("sp", "d_abp", [], [("abp",)], lambda e: [
            e.dma_start(out=lng_bc[:], in_=lng_d[i].partition_broadcast(128)),
            e.dma_start(out=lnb_bc[:], in_=lnb_d[i].partition_broadcast(128)),
            e.dma_start(out=bsrow[:], in_=bs_d[i:i + 1]),
        ])
        S.dma("pool", "d_ws", [], [("wmT",), ("wmTs",)], lambda e: [
            e.dma_start(out=wmT[:], in_=wsT_d[i].rearrange("h s t -> s h t")),
            e.dma_start(out=wmTs[:], in_=wsTs_d[i].rearrange("h s t -> s h t")),
        ])
        mbc = maskT[:].unsqueeze(1).to_broadcast([128, 4, 128])
        S.op("dve", [("const",), ("wmT",)], [("wmT",)],
             lambda e: e.tensor_tensor(out=wmT[:], in0=wmT[:], in1=mbc, op=ALU.mult))
        mbs = bdmask[:].unsqueeze(1).to_broadcast([64, 4, 64])
        S.op("dve", [("const",), ("wmTs",)], [("wmTs",)],
             lambda e: e.tensor_tensor(out=wmTs[:], in0=wmTs[:], in1=mbs, op=ALU.mult))
        S.op("dve", [("abp",)], [("bs_hi",)], lambda e: e.tensor_copy(bs_hi[:], bsrow[:]))
        S.op("dve", [("bs_hi",), ("abp",)], [("bs_lo",)],
             lambda e: e.tensor_tensor(out=bs_lo[:], in0=bsrow[:], in1=bs_hi[:], op=ALU.subtract))

    def ab_layer(i, layer, s_in, s_out):
        off = 0
        sq, off = wview(off, [4, 256], BF16)
        rt, off = wview(off, [256], F32)
        rstd = rt
        fbuf, off = wview(off, [4, NSEQ * 19], F32)
        a_ext, _ = wview(off, [4, 15 + 256], F32)
        as_ext, off = wview(off, [4, NSEQ, 19], F32)
        pa, off = wview(off, [NSEQ * 19], F32)
        pb, off = wview(off, [NSEQ * 19], F32)
        pooled0, off = wview(off, [4, 256], BF16)
        pooled1, off = wview(off, [4, 256], BF16)
        pooleds = [pooled0, pooled1]
        u, off = wview(off, [4, 256], F32)
        vf, off = wview(off, [2, 512], F32)
        vnb, off = wview(off, [2, 512], BF16)
        st6, off = wview(off, [2, 6], F32)
        mv, off = wview(off, [2, 2], F32)
        vr, off = wview(off, [2, 2], F32)
        off = (off + 31) // 32 * 32
        a_cmp, off = wview(off, [4, NS], F32)
        w_in = rview(s_in, 0, 8, 1536)
        pw = rview(s_in, 12288, 4, 128)
        ginx = PC_NMIX + layer * 8

        AEXT_KEYS = [("a_halo",)] + [("a_new", c) for c in range(4)]

        def load_sample_hist():
            with nc.allow_non_contiguous_dma(reason="small history rows"):
                S.dma("sp", "d_hist_ab", [], [("as_hist",)], lambda e: [
                    e.dma_start(out=as_ext[:, c, :, 0:15], in_=spT[i, c * 128:(c + 1) * 128]) for c in range(4)],
                    after=AEXT_KEYS)
        S.dma("sp", "d_old", [], [], lambda e: [e.dma_start(out=nps_old[i], in_=sp_tm[i, :, 4:15, :])])
        S.op("pool", [], [("a_halo",)], lambda e: e.memset(a_ext[:, :, 0:15], 0.0))

        nb = {}

        def norm_sq(tl, hf=0):
            n = tl.n
            S.op("act", tl.xk(), [("sq",)],
                 lambda e: e.activation(out=sq[:, :, :n], in_=X[:, hf * 4:hf * 4 + 4, tl.x0:tl.x0 + n], func=AF.Square))

        def norm_ss(tl, hf=0):
            n = tl.n
            if hf == 0:
                nb[tl.x0] = bank()
            b = nb[tl.x0]

            def mm(e):
                for c in range(4):
                    ins = e.matmul(psum[:, b, :n], lhsT=ones_b[:], rhs=sq[:, c, :n], start=(hf == 0 and c == 0),
                                   stop=(hf == 1 and c == 3), skip_group_check=True)
                return ins
            S.op("pe", [("sq",), ("ones",)] + ([("ps", b)] if hf == 1 else []), [("ps", b)], mm)

        def norm_sqrt(tl):
            n = tl.n
            b = nb[tl.x0]
            S.op("act", [("ps", b), ("eps",)], [("rt",)],
                 lambda e: e.activation(out=rt[:, :n], in_=psum[:, b, :n], func=AF.Sqrt, scale=1.0 / D,
                                        bias=eps_t[:, 0:1]))

        def norm_fin(tl, gidx=None):
            n = tl.n
            S.op("dve", [("rt",)], [("rt",)], lambda e: e.reciprocal(out=rstd[:, :n], in_=rt[:, :n]))
            norm_apply(tl, rstd, ginx if gidx is None else gidx, lambda c: A[:, c, tl.a0:tl.a0 + n],
                       lambda c: tl.ak(c), rkey=("rt",))

        def ffn_norm_first():
            for st in (XT256[0], XT256[1]):
                norm_sq(st, 0)
                norm_ss(st, 0)
                norm_sq(st, 1)
                norm_ss(st, 1)
                norm_sqrt(st)
                norm_fin(st, PC_NFFN + layer * 8)

        def stage_z(tl, pi, nxt_tl, sq0_done=False):
            n = tl.n
            pooled = pooleds[pi]
            hk = tl.ak() + slot_keys(s_in, 1)
            hkv = tl.ak() + slot_keys(s_in, 0)
            nsub = max(1, n // 128)
            m = min(128, n)
            if nxt_tl is not None:
                if not sq0_done:
                    norm_sq(nxt_tl, 0)
                norm_ss(nxt_tl, 0)
                norm_sq(nxt_tl, 1)
            for sbk in range(nsub):
                b = bank()

                def mm(e, sbk=sbk, b=b):
                    for k in range(8):
                        ins = e.matmul(psum[:m, b, :], lhsT=A[:, k, tl.a0 + sbk * 128:tl.a0 + sbk * 128 + m],
                                       rhs=w_in[:, k, 1024:1536], start=(k == 0), stop=(k == 7))
                    return ins
                S.op("pe", hkv, [("ps", b)], mm)
                S.op("act", [("ps", b)], [("vf", sbk)],
                     lambda e, b=b, sbk=sbk: e.activation(out=vf[:m, sbk, :], in_=psum[:m, b, :], func=AF.Gelu))
                S.op("dve", [("vf", sbk)], [("st6", sbk)], lambda e, sbk=sbk: e.bn_stats(st6[:m, sbk, :], vf[:m, sbk, :]))
                S.op("dve", [("st6", sbk)], [("mv", sbk)], lambda e, sbk=sbk: e.bn_aggr(mv[:m, sbk, :], st6[:m, sbk, :]))
            for c in range(4):
                b = bank()

                def mm(e, c=c, b=b):
                    for k in range(8):
                        ins = e.matmul(psum[:, b, :n], lhsT=w_in[:, k, c * 128:(c + 1) * 128],
                                       rhs=A[:, k, tl.a0:tl.a0 + n], start=(k == 0), stop=(k == 7))
                    return ins
                S.op("pe", hk, [("ps", b)], mm)
                if tl.sample:
                    S.op("act", [("ps", b)], [("as_new", c)],
                         lambda e, c=c, b=b: e.activation(out=as_ext[:, c, :, 15:19],
                                                          in_=psum[:, b, :n].rearrange("p (s j) -> p s j", j=4),
                                                          func=AF.Copy), after=AEXT_KEYS)
                    S.op("act", [("ps", b)], [("a_cmp", c)],
                         lambda e, c=c, b=b: e.activation(out=a_cmp[:, c, :n], in_=psum[:, b, :n], func=AF.Copy))
                else:
                    S.op("act", [("ps", b)], [("a_new", c)],
                         lambda e, c=c, b=b: e.activation(out=a_ext[:, c, 15:15 + n], in_=psum[:, b, :n], func=AF.Copy))
            deferred = []
            for g in range(4):
                if tl.sample:
                    src = as_ext[:, g]
                    L = 19

                    def sl(ap, lo, hi):
                        return ap[:, :, lo:hi]
                    bufs = [pa[:, 0:NSEQ * 19].rearrange("p (s l) -> p s l", l=19),
                            pb[:, 0:NSEQ * 19].rearrange("p (s l) -> p s l", l=19)]
                    fin = fbuf[:, g, 0:NSEQ * 19].rearrange("p (s l) -> p s l", l=19)
                    rk = [("as_hist",), ("as_new", g)]
                else:
                    src = a_ext[:, g]
                    L = 15 + n

                    def sl(ap, lo, hi):
                        return ap[:, lo:hi]
                    bufs = [pa, pb]
                    fin = fbuf[:, g, :]
                    rk = [("a_halo",), ("a_new", g)]
                cur = src
                ckey = None
                sh = 1
                for step in range(g + 1):
                    last = (step == g)
                    dst = fin if last else bufs[step % 2]
                    dkey = ("fbuf", g) if last else ("pbuf", step % 2)
                    lo = 2 * sh - 1
                    S.op("pool", rk if step == 0 else [ckey], [dkey],
                         lambda e, cur=cur, dst=dst, sh=sh, sl=sl, L=L, lo=lo: e.tensor_tensor(
                             out=sl(dst, lo, L), in0=sl(cur, lo, L), in1=sl(cur, lo - sh, L - sh), op=ALU.add))
                    cur, ckey = dst, dkey
                    sh *= 2
                w = 2 ** (g + 1)

                def finish(g=g, w=w, fin=fin, rk=rk):
                    fk = ("fbuf", g)
                    if tl.sample:
                        S.op("dve", [fk] + rk, [("pooled", pi, g)],
                             lambda e: e.scalar_tensor_tensor(
                                 out=pooled[:, g, :n].rearrange("p (s j) -> p s j", j=4), in0=fin[:, :, 15:19],
                                 scalar=1.0 / w, in1=as_ext[:, g, :, 15:19], op0=ALU.mult, op1=ALU.subtract))
                    else:
                        S.op("dve", [fk] + rk, [("pooled", pi, g)],
                             lambda e: e.scalar_tensor_tensor(
                                 out=pooled[:, g, :n], in0=fin[:, 15:15 + n], scalar=1.0 / w, in1=a_ext[:, g, 15:15 + n],
                                 op0=ALU.mult, op1=ALU.subtract))
                        if tl.x0 == 0:
                            S.op("dve", [fk, ("const",)], [fk],
                                 lambda e: e.tensor_tensor(out=fin[:, 15:31], in0=fin[:, 15:31], in1=invcnt[:, g, :],
                                                           op=ALU.mult))
                            S.op("dve", [fk] + rk + [("pooled", pi, g)], [("pooled", pi, g)],
                                 lambda e: e.tensor_tensor(out=pooled[:, g, 0:16], in0=fin[:, 15:31],
                                                           in1=a_ext[:, g, 15:31], op=ALU.subtract))
                deferred.append(finish)
            if tl.sample:
                S.dma("sp", "d_st", [("a_cmp", c) for c in range(4)], [], lambda e: [
                    e.dma_start(out=nps_new[i].rearrange("(c p) t -> p c t", p=128), in_=a_cmp[:, :, :])])
            else:
                if tl.x0 + n == SEQ:
                    with nc.allow_non_contiguous_dma(reason="small state rows"):
                        S.dma("sp", "d_st", [("a_new", c) for c in range(4)], [], lambda e: [
                            e.dma_start(out=npp[i].rearrange("(c p) r -> p c r", p=128), in_=a_ext[:, :, n:n + 15])])
                else:
                    S.op("pool", [("a_new", c) for c in range(4)], [("a_halo",)],
                         lambda e: e.tensor_copy(a_ext[:, :, 0:15], a_ext[:, :, n:n + 15]))
            if nxt_tl is not None:
                norm_ss(nxt_tl, 1)
            S.op("act", [("mv", s_) for s_ in range(nsub)] + [("eps",)], [("vr0",)],
                 lambda e: e.activation(out=vr[:m, 0, 0:nsub], in_=mv[:m, 0:nsub, 1], func=AF.Sqrt, bias=eps_t[:m, 0:1]))
            if nxt_tl is not None:
                norm_sqrt(nxt_tl)
            S.op("dve", [("vr0",)], [("vr1",)], lambda e: e.reciprocal(out=vr[:m, 1, 0:nsub], in_=vr[:m, 0, 0:nsub]))
            for sbk in range(nsub):
                S.op("dve", [("vf", sbk), ("mv", sbk), ("abp",)], [("vf", sbk)],
                     lambda e, sbk=sbk: e.scalar_tensor_tensor(out=vf[:m, sbk, :], in0=vf[:m, sbk, :],
                                                               scalar=mv[:m, sbk, 0:1], in1=lng_bc[:m, :],
                                                               op0=ALU.subtract, op1=ALU.mult))
                if tl.sample:
                    S.op("dve", [("vf", sbk), ("vr1",), ("abp",)], [("vf", sbk)],
                         lambda e, sbk=sbk: e.scalar_tensor_tensor(out=vf[:m, sbk, :], in0=vf[:m, sbk, :],
                                                                   scalar=vr[:m, 1, sbk:sbk + 1], in1=lnb_bc[:m, :],
                                                                   op0=ALU.mult, op1=ALU.add))
                    S.op("dve", [("vf", sbk)], [("vnb", sbk)],
                         lambda e, sbk=sbk: e.tensor_copy(vnb[:m, sbk, :], vf[:m, sbk, :]))
                    S.dma("sp", "d_gv", [("vf", sbk)], [], lambda e, sbk=sbk: [e.dma_start(out=gv[i], in_=vf[:m, sbk, :])])
                else:
                    S.op("dve", [("vf", sbk), ("vr1",), ("abp",)], [("vnb", sbk)],
                         lambda e, sbk=sbk: e.scalar_tensor_tensor(out=vnb[:m, sbk, :], in0=vf[:m, sbk, :],
                                                                   scalar=vr[:m, 1, sbk:sbk + 1], in1=lnb_bc[:m, :],
                                                                   op0=ALU.mult, op1=ALU.add))
            if nxt_tl is not None:
                norm_fin(nxt_tl)
            for c in range(4):
                b = bank()

                def mm(e, c=c, b=b):
                    for k in range(8):
                        ins = e.matmul(psum[:, b, :n], lhsT=w_in[:, k, 512 + c * 128:512 + (c + 1) * 128],
                                       rhs=A[:, k, tl.a0:tl.a0 + n], start=(k == 0), stop=(k == 7))
                    return ins
                S.op("pe", hk, [("ps", b)], mm)
                S.op("act", [("ps", b)], [("u", c)],
                     lambda e, c=c, b=b: e.activation(out=u[:, c, :n], in_=psum[:, b, :n], func=AF.Gelu))
            return deferred

        def stage_pm(tl, pi):
            n = tl.n
            pooled = pooleds[pi]
            for g in range(4):
                b = bank()
                S.op("pe", [("pooled", pi, g)] + slot_keys(s_in, 2), [("ps", b)],
                     lambda e, g=g, b=b: e.matmul(psum[:, b, :n], lhsT=pw[:, g, :], rhs=pooled[:, g, :n],
                                                  start=True, stop=True))
                S.op("act", [("ps", b), ("const",)], tl.ak(g),
                     lambda e, g=g, b=b: e.activation(out=A[:, g, tl.a0:tl.a0 + n], in_=psum[:, b, :n], func=AF.Identity,
                                                      scale=pcols[:, PC_PSCALE + i * 4 + g:PC_PSCALE + i * 4 + g + 1]))

        def stage_gate(tl):
            n = tl.n
            nsub = max(1, n // 128)
            for hd in range(4):
                b = bank()

                def mm(e, hd=hd, b=b):
                    if tl.sample:
                        o3 = psum[:, b, :n].rearrange("p (s j) -> p s j", j=4)
                        e.matmul(o3, lhsT=ones_b[0:1, :], rhs=bs_hi[0:1, hd, 0:4].unsqueeze(1).to_broadcast([1, NSEQ, 4]),
                                 start=True, stop=False, skip_group_check=True)
                        e.matmul(o3, lhsT=ones_b[0:1, :], rhs=bs_lo[0:1, hd, 0:4].unsqueeze(1).to_broadcast([1, NSEQ, 4]),
                                 start=False, stop=False, skip_group_check=True)
                        ins = e.matmul(psum[:, b, :n], lhsT=vnb[:n, 0, hd * 128:(hd + 1) * 128], rhs=wmTs[:, hd, :],
                                       start=False, stop=True, skip_group_check=True)
                    else:
                        first = True
                        for sbk in range(nsub):
                            cols = psum[:, b, sbk * 128:(sbk + 1) * 128]
                            e.matmul(cols, lhsT=ones_b[0:1, :], rhs=bs_hi[0:1, hd, :], start=first, stop=False,
                                     skip_group_check=True)
                            first = False
                            e.matmul(cols, lhsT=ones_b[0:1, :], rhs=bs_lo[0:1, hd, :], start=False, stop=False,
                                     skip_group_check=True)
                            ins = e.matmul(cols, lhsT=vnb[:, sbk, hd * 128:(hd + 1) * 128], rhs=wmT[:, hd, :],
                                           start=False, stop=(sbk == nsub - 1), skip_group_check=True)
                    return ins
                S.op("pe", [("vnb", s_) for s_ in range(nsub)] + [("wmT",), ("wmTs",), ("bs_hi",), ("bs_lo",), ("ones",)],
                     [("ps", b)], mm)
                S.op("dve", [("ps", b), ("u", hd)], tl.ak(4 + hd),
                     lambda e, hd=hd, b=b: e.tensor_tensor(out=A[:, 4 + hd, tl.a0:tl.a0 + n], in0=psum[:, b, :n],
                                                           in1=u[:, hd, :n], op=ALU.mult))

        subs = XT256
        norm_sq(subs[0], 0)
        norm_ss(subs[0], 0)
        norm_sq(subs[0], 1)
        norm_ss(subs[0], 1)
        norm_sqrt(subs[0])
        norm_fin(subs[0])
        prev = None
        ppend = []
        for idx, tl in enumerate(subs):
            if tl.sample:
                load_sample_hist()
            deferred = stage_z(tl, idx % 2, subs[idx + 1] if idx + 1 < len(subs) else None, sq0_done=(idx > 0))
            while ppend:
                proj_out(ppend.pop(0), s_out, 0)
            stage_gate(tl)
            for fn_ in deferred:
                fn_()
            if idx + 2 < len(subs):
                norm_sq(subs[idx + 2], 0)
            if idx == len(subs) - 2:
                ffn_norm_first()
            if prev is not None:
                stage_pm(prev, (idx - 1) % 2)
                if prev.x0 % 512 == 256:
                    ppend.append(XT512[prev.x0 // 512])
            prev = tl
        stage_pm(prev, (len(subs) - 1) % 2)
        while ppend:
            proj_out(ppend.pop(0), s_out, 0)
        proj_out(XT512[4], s_out, 0)

    def c_layer(i, layer, s_in, load_pw2):
        ginx = PC_NMIX + layer * 8
        w1 = rview(s_in, 0, 8, 2048)
        other = 1 - s_in
        off = 0
        gs_ext, off = wview(off, [8, NSEQ, 34], BF16)
        sq, off = wview(off, [8, 256], BF16)
        rt, off = wview(off, [256], F32)
        rstd, off = wview(off, [256], F32)
        h0, off = wview(off, [8, 256], BF16)
        h1, off = wview(off, [8, 256], BF16)
        hbuf = [h0, h1]
        sg0, off = wview(off, [256], F32)
        sg1, off = wview(off, [256], F32)
        gl0, off = wview(off, [256], F32)
        gl1, off = wview(off, [256], F32)
        sgs, gls = [sg0, sg1], [gl0, gl1]
        dg = {0: rview(other, 0, 4 * CW, 128), 1: rview(s_in, 0, 4 * CW, 128)}

        def build_diag_chunk(c):
            half, cc = c // 4, c % 4
            sl_ = other if half == 0 else s_in
            wcol = pcols[:, PC_WDW + (i * 8 + c) * CW:PC_WDW + (i * 8 + c + 1) * CW]
            S.op("dve", [("const",), ("identb",)], [("diag", c)],
                 lambda e: e.tensor_tensor(out=dg[half][:, cc * CW:(cc + 1) * CW, :],
                                           in0=ident_b[:].unsqueeze(1).to_broadcast([128, CW, 128]),
                                           in1=wcol.unsqueeze(2).to_broadcast([128, CW, 128]), op=ALU.mult),
                 after=slot_keys(sl_))

        with nc.allow_non_contiguous_dma(reason="small history rows"):
            S.dma("pool", "d_hist_c", [], [("gs_hist",)], lambda e: [
                e.dma_start(out=gs_ext[:, c, :, 0:30], in_=scT[i, c * 128:(c + 1) * 128]) for c in range(8)])
        S.dma("sp", "d_old", [], [], lambda e: [e.dma_start(out=ncs_old[i], in_=sc_tm[i, :, 4:30, :])])

        def c_norm_sq(idx):
            tl = XT256[idx]
            S.op("act", tl.xk(), [("sq",)],
                 lambda e: e.activation(out=sq[:, :, :tl.n], in_=X[:, :, tl.x0:tl.x0 + tl.n], func=AF.Square))

        def c_norm_rest(idx):
            tl = XT256[idx]
            n_ = tl.n
            hh = hbuf[idx % 2]
            b = bank()

            def mm(e):
                for c in range(8):
                    ins = e.matmul(psum[:, b, :n_], lhsT=ones_b[:], rhs=sq[:, c, :n_], start=(c == 0), stop=(c == 7))
                return ins
            S.op("pe", [("sq",), ("ones",)], [("ps", b)], mm)
            S.op("act", [("ps", b), ("eps",)], [("rt",)],
                 lambda e: e.activation(out=rt[:, :n_], in_=psum[:, b, :n_], func=AF.Sqrt, scale=1.0 / D,
                                        bias=eps_t[:, 0:1]))
            S.op("dve", [("rt",)], [("rstd",)], lambda e: e.reciprocal(out=rstd[:, :n_], in_=rt[:, :n_]))

        def c_norm_app(idx, chunks):
            tl = XT256[idx]
            hh = hbuf[idx % 2]
            norm_apply(tl, rstd, ginx, lambda c: hh[:, c, :tl.n], lambda c: [("h", idx % 2, c)], chunks=chunks)

        c_norm_sq(0)
        c_norm_rest(0)
        c_norm_app(0, range(8))
        c_norm_sq(1)
        for idx, tl in enumerate(XT256):
            n = tl.n
            if 1 <= idx <= 4:
                build_diag_chunk(idx - 1)
            h = hbuf[idx % 2]
            hk = [("h", idx % 2, c) for c in range(8)] + slot_keys(s_in)
            for c in range(8):
                if idx + 1 < len(XT256):
                    if c == 0:
                        c_norm_rest(idx + 1)
                    elif 1 <= c <= 4:
                        c_norm_app(idx + 1, range(2 * (c - 1), 2 * (c - 1) + 2))
                    elif c == 5 and idx + 2 < len(XT256):
                        c_norm_sq(idx + 2)
                b1, b2 = bank(), bank()

                def mm(e, c=c, b1=b1, b2=b2):
                    for k in range(8):
                        e.matmul(psum[:, b1, :n], lhsT=w1[:, k, c * 128:(c + 1) * 128], rhs=h[:, k, :n],
                                 start=(k == 0), stop=(k == 7))
                    for k in range(8):
                        ins = e.matmul(psum[:, b2, :n], lhsT=w1[:, k, 1024 + c * 128:1024 + (c + 1) * 128],
                                       rhs=h[:, k, :n], start=(k == 0), stop=(k == 7))
                    return ins
                S.op("pe", hk, [("ps", b1), ("ps", b2)], mm)
                sg, gl = sgs[c % 2], gls[c % 2]
                S.op("act", [("ps", b2)], [("sg", c % 2)],
                     lambda e, b2=b2, sg=sg: e.activation(out=sg[:, :n], in_=psum[:, b2, :n], func=AF.Sigmoid))
                S.op("dve", [("ps", b1), ("sg", c % 2)], [("gl", c % 2)],
                     lambda e, b1=b1, sg=sg, gl=gl: e.tensor_tensor(out=gl[:, :n], in0=psum[:, b1, :n], in1=sg[:, :n],
                                                                    op=ALU.mult))
                if tl.sample:
                    S.op("act", [("gl", c % 2)], [("gs_new", c)],
                         lambda e, c=c, gl=gl: e.activation(out=gs_ext[:, c, :, 30:34],
                                                            in_=gl[:, :n].rearrange("p (s j) -> p s j", j=4),
                                                            func=AF.Copy))
                    S.dma("sp", "d_sg%d" % (c % 2), [("gl", c % 2)], [],
                          lambda e, c=c, gl=gl: [e.dma_start(out=ncs_new[i, c * 128:(c + 1) * 128, :], in_=gl[:, :n])])
                else:
                    S.op("act", [("gl", c % 2)], tl.ak(c),
                         lambda e, c=c, gl=gl: e.activation(out=A[:, c, tl.a0:tl.a0 + n], in_=gl[:, :n], func=AF.Copy))
                    if tl.x0 + n == SEQ:
                        with nc.allow_non_contiguous_dma(reason="small state rows"):
                            S.dma("sp", "d_sg%d" % (c % 2), [("gl", c % 2)], [],
                                  lambda e, c=c, gl=gl: [e.dma_start(out=ncp[i, c * 128:(c + 1) * 128, :],
                                                                     in_=gl[:, n - 30:n])])
        for c in range(4, 8):
            build_diag_chunk(c)
        snap = S.fence_snapshot()
        S.fence_apply(snap)

        off = 8 * NSEQ * 34 * 2
        cv, off = wview(off, [8, 256], F32)
        cvb, off = wview(off, [8, 256], BF16)
        sqb, off = wview(off, [8, 256], BF16)
        mean, off = wview(off, [256], F32)
        var, off = wview(off, [256], F32)
        rs2, off = wview(off, [256], F32)
        order = [XT256[s] for s in range(7, -1, -1)] + [XT256[8]]
        pending = []
        for tl in order:
            n = tl.n
            if tl.sample:
                gkeys = [("gs_hist",)] + [("gs_new", c) for c in range(8)]
            else:
                s_ = tl.x0 // 256
                gkeys = tl.ak() + ([("A", s_ - 1, c) for c in range(8)] if s_ > 0 else [("A", "pad", c) for c in range(8)])
            for c in range(8):
                b = bank()
                half, cc = c // 4, c % 4

                def mm(e, c=c, b=b, half=half, cc=cc):
                    for k in range(CW):
                        if tl.sample:
                            rhs = gs_ext[:, c, :, k:k + 4]
                            o = psum[:, b, :n].rearrange("p (s j) -> p s j", j=4)
                        else:
                            rhs = A[:, c, tl.x0 + k:tl.x0 + k + n]
                            o = psum[:, b, :n]
                        ins = e.matmul(o, lhsT=dg[half][:, cc * CW + k, :], rhs=rhs, start=(k == 0), stop=(k == CW - 1))
                    return ins
                S.op("pe", gkeys + [("diag", c)] + slot_keys(0) + slot_keys(1), [("ps", b)], mm)
                if c == 0 and pending:
                    pending.pop()()
                S.op("act", [("ps", b), ("const",)], [("cv", c)],
                     lambda e, c=c, b=b: e.activation(out=cv[:, c, :n], in_=psum[:, b, :n], func=AF.Identity,
                                                      bias=pcols[:, PC_BDW + i * 8 + c:PC_BDW + i * 8 + c + 1]))
                S.op("act", [("cv", c)], [("sqb", c)],
                     lambda e, c=c: e.activation(out=sqb[:, c, :n], in_=cv[:, c, :n], func=AF.Square))
                S.op("act", [("cv", c)], [("cvb", c)],
                     lambda e, c=c: e.activation(out=cvb[:, c, :n], in_=cv[:, c, :n], func=AF.Copy))
            def post(tl=tl, n=n):
              b1, b2 = bank(), bank()

              def mm2(e, b1=b1, b2=b2):
                  for c in range(8):
                      e.matmul(psum[:, b1, :n], lhsT=ones_b[:], rhs=cvb[:, c, :n], start=(c == 0), stop=(c == 7))
                  for c in range(8):
                      ins = e.matmul(psum[:, b2, :n], lhsT=ones_b[:], rhs=sqb[:, c, :n], start=(c == 0), stop=(c == 7))
                  return ins
              S.op("pe", [("cvb", c) for c in range(8)] + [("sqb", c) for c in range(8)] + [("ones",)],
                   [("ps", b1), ("ps", b2)], mm2)
              S.op("dve", [("ps", b1)], [("mean",)],
                   lambda e, b1=b1: e.tensor_scalar(out=mean[:, :n], in0=psum[:, b1, :n], scalar1=1.0 / D, scalar2=None,
                                                    op0=ALU.mult))
              S.op("dve", [("mean",)], [("var",)],
                   lambda e: e.tensor_tensor(out=var[:, :n], in0=mean[:, :n], in1=mean[:, :n], op=ALU.mult))
              S.op("dve", [("ps", b2), ("var",)], [("var",)],
                   lambda e, b2=b2: e.scalar_tensor_tensor(out=var[:, :n], in0=psum[:, b2, :n], scalar=1.0 / D,
                                                           in1=var[:, :n], op0=ALU.mult, op1=ALU.subtract))
              S.op("act", [("var",), ("eps",)], [("rs2",)],
                   lambda e: e.activation(out=rs2[:, :n], in_=var[:, :n], func=AF.Sqrt, bias=eps_t[:, 0:1]))
              S.op("dve", [("rs2",)], [("rs2",)], lambda e: e.reciprocal(out=rs2[:, :n], in_=rs2[:, :n]))
              cvk = [("cv", c) for c in range(8)]
              S.op("dve", cvk + [("mean",)], cvk,
                   lambda e: e.tensor_tensor(out=cv[:, :, :n], in0=cv[:, :, :n],
                                             in1=mean[:, :n].unsqueeze(1).to_broadcast([128, 8, n]), op=ALU.subtract))
              S.op("dve", cvk + [("rs2",)], cvk,
                   lambda e: e.tensor_tensor(out=cv[:, :, :n], in0=cv[:, :, :n],
                                             in1=rs2[:, :n].unsqueeze(1).to_broadcast([128, 8, n]), op=ALU.mult))
              for c in range(8):
                  S.op("act", [("cv", c), ("const",)], tl.ak(c),
                       lambda e, c=c: e.activation(out=A[:, c, tl.a0:tl.a0 + n], in_=cv[:, c, :n], func=AF.Silu,
                                                   scale=pcols[:, PC_LNG + i * 8 + c:PC_LNG + i * 8 + c + 1],
                                                   bias=pcols[:, PC_LNB + i * 8 + c:PC_LNB + i * 8 + c + 1]))
            pending.append(post)
        while pending:
            pending.pop()()
        s_out = load_pw2()
        snap2 = S.fence_snapshot()
        S.fence_apply(snap2)
        off = 0
        sq_f, off = wview(off, [8, 256], BF16)
        rt_f, off = wview(off, [256], F32)
        rstd_f, off = wview(off, [256], F32)
        for k_, tl in enumerate(C_ORDER):
            proj_out(tl, s_out, 0)
            if k_ == 1:
                t5 = C_ORDER[0]
                for st in (XT256[t5.x0 // 256], XT256[t5.x0 // 256 + 1]):
                    norm_stats(st, sq_f, rt_f, rstd_f)
                    norm_apply(st, rstd_f, PC_NFFN + layer * 8,
                               lambda c, st=st: A[:, c, st.a0:st.a0 + st.n], lambda c, st=st: st.ak(c))
        return s_out


    def unit_ab_in(i):
        wv = w_in_d[i].rearrange("(kc p) n -> p kc n", p=128)
        return [(0, (8, 1536), wv[:, :, 1024:1536], (1024, 1536)),
                (0, (8, 1536), wv[:, :, 0:1024], (0, 1024)),
                (12288, (4, 128), pool_w_d[i].rearrange("g c d -> c g d"))]

    def unit_w(dram):
        return [(0, (8, D), dram.rearrange("(kc p) n -> p kc n", p=128))]

    work_fence = [None]
    C_ORDER = [XT512[3], XT512[2], XT512[1], XT512[0], XT512[4]]

    def mixer_unit(layer):
        i = layer // 2
        if layer % 2 == 0:
            return load_unit(unit_ab_in(i))
        return load_unit([(0, (8, 2048), w_pw1_d[i].rearrange("(kc p) n -> p kc n", p=128))])

    yv = yT.rearrange("(c p) t -> p c t", p=128)
    ydma = [0]

    def final_norm_tile(tl, sq, rt, rstd):
        subs = [tl] if tl.sample else [XT256[tl.x0 // 256], XT256[tl.x0 // 256 + 1]]
        for st in subs:
            n = st.n
            norm_stats(st, sq, rt, rstd)
            for c in range(8):
                S.op("dve", st.xk(c) + [("rstd",), ("const",)], st.xk(c),
                     lambda e, c=c, st=st, n=n: e.scalar_tensor_tensor(
                         out=X[:, c, st.x0:st.x0 + n], in0=X[:, c, st.x0:st.x0 + n],
                         scalar=pcols[:, PC_NFIN + c:PC_NFIN + c + 1], in1=rstd[:, :n], op0=ALU.mult, op1=ALU.mult))
            ydma[0] += 1
            S.dma("sp", "d_y%d" % (ydma[0] % 4), st.xk(), [],
                  lambda e, st=st, n=n: [e.dma_start(out=yv[:, :, st.x0:st.x0 + n], in_=X[:, :, st.x0:st.x0 + n])])

    nxt = {}
    nxt[0] = mixer_unit(0)
    for layer in range(depth_run):
        i = layer // 2
        if work_fence[0] is not None:
            S.fence_apply(work_fence[0])
        if layer % 2 == 0:
            ab_params(i)
            s_in = nxt[layer]
            s_out = load_unit(unit_w(w_out_d[i]))
            ab_layer(i, layer, s_in, s_out)
        else:
            s_in = nxt[layer]
            c_layer(i, layer, s_in, lambda i=i: load_unit(unit_w(w_pw2_d[i])))
        work_fence[0] = S.fence_snapshot()
        pre = ffn_preload(layer)
        S.fence_apply(work_fence[0])

        def next_loader(layer=layer):
            if layer + 1 < depth_run:
                nxt[layer + 1] = mixer_unit(layer + 1)
        ffn_layer(layer, next_loader, final_norm_tile if layer == depth_run - 1 else None, pre_slots=pre,
                  tiles=(XT512 if layer % 2 == 0 else C_ORDER), first_norm_done=True)
        work_fence[0] = S.fence_snapshot()

    S.final_wait("sp")
    return nc


_PROGRAM = {}


def _consts():
    s = np.arange(128)
    maskT = (s[:, None] <= s[None, :]).astype(np.float32)
    q = np.arange(64)
    bd = ((q[:, None] // 4 == q[None, :] // 4) & (q[:, None] % 4 <= q[None, :] % 4)).astype(np.float32)
    inv = np.zeros((128, 4, 16), np.float32)
    for g, w in enumerate((2, 4, 8, 16)):
        inv[:, g, :] = 1.0 / np.minimum(np.arange(16) + 1, w)
    return maskT, bd, inv


def _col(v):
    v = np.asarray(v, np.float32)
    lead = v.shape[:-1]
    C = v.shape[-1] // 128
    return np.moveaxis(v.reshape(*lead, C, 128), -1, 0).reshape(128, -1)


def kernel(x_prompt, x_sample, state_pool, state_conv, norm_mix, norm_ffn, norm_final,
           ab_w_in, ab_pool_w, ab_pool_scale, ab_ln_g, ab_ln_b, ab_ws, ab_bs, ab_w_out,
           c_w_pw1, c_w_dw, c_b_dw, c_ln_g, c_ln_b, c_w_pw2, ffn_w1, ffn_w2, _depth_run=DEPTH):
    f = lambda a: np.ascontiguousarray(np.asarray(a, dtype=np.float32))
    x_prompt, x_sample, state_pool, state_conv = f(x_prompt), f(x_sample), f(state_pool), f(state_conv)
    maskT, bd, inv = _consts()
    wdwT = np.transpose(f(c_w_dw), (0, 2, 1))
    wdw_cols = wdwT.reshape(2, 8, 128, CW).transpose(2, 0, 1, 3).reshape(128, -1)
    pcols = np.concatenate([_col(f(norm_mix)), _col(f(norm_ffn)), _col(f(norm_final)), _col(f(ab_pool_scale)),
                            _col(f(c_b_dw)), _col(f(c_ln_g)), _col(f(c_ln_b)), wdw_cols], axis=1)
    assert pcols.shape == (128, PC_N), pcols.shape
    wsT = np.ascontiguousarray(np.transpose(f(ab_ws), (0, 1, 3, 2)))
    wsTs = np.ascontiguousarray(np.tile(wsT[:, :, :4, :4], (1, 1, 16, 16)))
    shared = {
        "pcols": np.ascontiguousarray(pcols), "ab_ln_g": f(ab_ln_g), "ab_ln_b": f(ab_ln_b), "wsT": wsT, "wsTs": wsTs,
        "ab_bs": f(ab_bs), "ident": np.eye(128, dtype=np.float32), "maskT": maskT, "bdmask": bd, "invcnt": inv,
        "ab_w_in": f(ab_w_in), "ab_pool_w": f(ab_pool_w), "ab_w_out": f(ab_w_out), "c_w_pw1": f(c_w_pw1),
        "c_w_pw2": f(c_w_pw2), "ffn_w1": f(ffn_w1), "ffn_w2": f(ffn_w2),
    }
    in_maps = []
    for c in range(NCORES):
        xs = x_sample[c * NSEQ:(c + 1) * NSEQ].reshape(NS, D)
        xT = np.ascontiguousarray(np.concatenate([x_prompt[c], xs], axis=0).T)
        sp = state_pool[:, c * NSEQ:(c + 1) * NSEQ]
        sc = state_conv[:, c * NSEQ:(c + 1) * NSEQ]
        m = dict(shared)
        m["xT"] = xT
        m["sp_tm"] = np.ascontiguousarray(sp)
        m["spT"] = np.ascontiguousarray(np.transpose(sp, (0, 3, 1, 2)))
        m["sc_tm"] = np.ascontiguousarray(sc)
        m["scT"] = np.ascontiguousarray(np.transpose(sc, (0, 3, 1, 2)))
        in_maps.append(m)
    if _depth_run not in _PROGRAM:
        _PROGRAM[_depth_run] = build_program(_depth_run)
    nc = _PROGRAM[_depth_run]
    res = run_bass_kernel_spmd(nc, in_maps, core_ids=list(range(NCORES)))
    R = res.results
    y_prompt = np.stack([R[c]["yT"][:, :SEQ].T for c in range(NCORES)])
    y_sample = np.concatenate([R[c]["yT"][:, SEQ:].T.reshape(NSEQ, 4, D) for c in range(NCORES)])
    npp = np.stack([np.transpose(R[c]["npp"], (0, 2, 1)) for c in range(NCORES)], axis=1)
    ncp = np.stack([np.transpose(R[c]["ncp"], (0, 2, 1)) for c in range(NCORES)], axis=1)

    def smp(old, new, width):
        outs = []
        for c in range(NCORES):
            o = R[c][old]
            nw = np.transpose(R[c][new], (0, 2, 1)).reshape(2, NSEQ, 4, width)
            outs.append(np.concatenate([o, nw], axis=2))
        return np.concatenate(outs, axis=1)
    nps = smp("nps_old", "nps_new", 512)
    ncs = smp("ncs_old", "ncs_new", D)
    gvs = np.concatenate([R[c]["gv"].reshape(2, NSEQ, 4, 512) for c in range(NCORES)], axis=1)
    return (np.ascontiguousarray(y_prompt), np.ascontiguousarray(y_sample), np.ascontiguousarray(npp),
            np.ascontiguousarray(nps), np.ascontiguousarray(ncp), np.ascontiguousarray(ncs),
            np.ascontiguousarray(gvs))
```
